# Optimizing a Trainium2 kernel written in Bass

```python
import math
import jax
import jax.numpy as jnp
from jax import lax
import numpy as np

D_MODEL = 2048
BATCH = 4
SEQ = 2048
DEPTH = 1
DEC_BATCH = 128
DEC_SEQ = 8
PAST_LEN = 16384
PAGE_SIZE = 128

N_META = 16
EPS = 1e-6
SSD_INNER = D_MODEL
SSD_HEADDIM = 64
SSD_HEADS = SSD_INNER // SSD_HEADDIM
SSD_GROUPS = 4
SSD_STATE = 128
SSD_CONV = 4
SSD_CHUNK = 128
SSD_CONV_DIM = SSD_INNER + 2 * SSD_GROUPS * SSD_STATE
POOL_WIDTH = D_MODEL
POOL_WINDOWS = (2, 4, 8, 16)
POOL_GROUPS = 4
POOL_GROUP_WIDTH = POOL_WIDTH // POOL_GROUPS
POOL_MAX = 16
N_BRANCH = 2
OFF_Z_SSD = 0
OFF_XBC = OFF_Z_SSD + SSD_INNER
OFF_DT = OFF_XBC + SSD_CONV_DIM
OFF_Z_POOL = OFF_DT + SSD_HEADS
OFF_U_POOL = OFF_Z_POOL + POOL_WIDTH
OFF_GATES = OFF_U_POOL + POOL_WIDTH
D_IN_PROJ = OFF_GATES + N_BRANCH * D_MODEL

kernel_name = "hybrid_ssd_pool_gated_decode_step"


def rms_norm(x, w):
    xf = x.astype(jnp.float32)
    xf = xf * lax.rsqrt(jnp.mean(xf * xf, axis=-1, keepdims=True) + EPS)
    return (xf * w.astype(jnp.float32)).astype(x.dtype)


def causal_dwconv(xbc, hist, w, b):
    ext = jnp.concatenate([hist.astype(xbc.dtype), xbc], axis=1)
    l = xbc.shape[1]
    acc = b
    for k in range(SSD_CONV):
        acc = acc + ext[:, k:k + l] * w[k]
    return jax.nn.silu(acc), ext[:, ext.shape[1] - (SSD_CONV - 1):]


def ssd_segment(x, dt, a, bm, cm, h0, q):
    bsz, l, n_h, p = x.shape
    g, n = bm.shape[2], bm.shape[3]
    r = n_h // g
    nc = l // q
    xr = x.reshape(bsz, nc, q, g, r, p)
    dtr = dt.reshape(bsz, nc, q, g, r)
    br = bm.reshape(bsz, nc, q, g, n)
    cr = cm.reshape(bsz, nc, q, g, n)
    a_cum = jnp.cumsum(dtr * a.reshape(g, r), axis=2)
    xdt = xr * dtr[..., None]
    causal = jnp.tril(jnp.ones((q, q), dtype=bool))
    seg = a_cum[:, :, :, None] - a_cum[:, :, None, :]
    decay = jnp.exp(jnp.where(causal[:, :, None, None], seg, -jnp.inf))
    cb = jnp.einsum('bctgn,bcsgn->bctsg', cr, br)
    y_diag = jnp.einsum('bctsg,bctsgr,bcsgrp->bctgrp', cb, decay, xdt)
    decay_end = jnp.exp(a_cum[:, :, -1:] - a_cum)
    chunk_states = jnp.einsum('bcsgn,bcsgr,bcsgrp->bcgrpn', br, decay_end, xdt)
    chunk_decay = jnp.exp(a_cum[:, :, -1])

    def step(h, inp):
        s_c, d_c = inp
        return h * d_c[..., None, None] + s_c, h

    h_final, h_in = lax.scan(step, h0.reshape(bsz, g, r, p, n),
                             (jnp.moveaxis(chunk_states, 1, 0), jnp.moveaxis(chunk_decay, 1, 0)))
    h_in = jnp.moveaxis(h_in, 0, 1)
    y_off = jnp.einsum('bctgn,bctgr,bcgrpn->bctgrp', cr, jnp.exp(a_cum), h_in)
    y = (y_diag + y_off).reshape(bsz, l, n_h, p)
    return y, h_final.reshape(bsz, n_h, p, n)


def pool_means(ext, n_new, first_pos):
    bsz, tot, c = ext.shape
    p = tot - n_new
    s = jnp.concatenate([jnp.zeros((bsz, POOL_MAX, c), jnp.float32),
                         jnp.cumsum(ext.astype(jnp.float32), axis=1)], axis=1)
    pos = first_pos + jnp.arange(n_new) + 1
    outs = []
    for gi, w in enumerate(POOL_WINDOWS):
        cs = slice(gi * POOL_GROUP_WIDTH, (gi + 1) * POOL_GROUP_WIDTH)
        upper = s[:, p + POOL_MAX:p + POOL_MAX + n_new, cs]
        lower = s[:, p + POOL_MAX - w:p + POOL_MAX - w + n_new, cs]
        count = jnp.minimum(w, pos).astype(jnp.float32)
        outs.append((upper - lower) / count[None, :, None])
    return jnp.concatenate(outs, axis=-1).astype(ext.dtype)


def mixer_layer(h, conv_hist, ssm_h0, pool_hist, first_pos, segments, lp):
    (norm_w, w_in, conv_w, conv_b, dt_bias, a_log, d_skip, ssd_norm_w, w_proj_ssd,
     pool_mix_w, pool_mix_b, pool_scale, w_proj_pool, w_out) = lp
    f32 = jnp.float32
    bsz, l, _ = h.shape
    xn = rms_norm(h, norm_w)
    proj = jnp.einsum('bld,de->ble', xn, w_in)
    z_ssd = proj[..., OFF_Z_SSD:OFF_XBC]
    xbc = proj[..., OFF_XBC:OFF_DT]
    dt_raw = proj[..., OFF_DT:OFF_Z_POOL]
    z_pool = proj[..., OFF_Z_POOL:OFF_U_POOL]
    u = proj[..., OFF_U_POOL:OFF_GATES]
    gate_ssd = jax.nn.sigmoid(proj[..., OFF_GATES:OFF_GATES + D_MODEL])
    gate_pool = jax.nn.sigmoid(proj[..., OFF_GATES + D_MODEL:D_IN_PROJ])

    xbc_act, conv_new = causal_dwconv(xbc, conv_hist, conv_w, conv_b)
    xbc_act = xbc_act.astype(f32)
    nb = SSD_GROUPS * SSD_STATE
    xs = xbc_act[..., :SSD_INNER].reshape(bsz, l, SSD_HEADS, SSD_HEADDIM)
    b_ssm = xbc_act[..., SSD_INNER:SSD_INNER + nb].reshape(bsz, l, SSD_GROUPS, SSD_STATE)
    c_ssm = xbc_act[..., SSD_INNER + nb:].reshape(bsz, l, SSD_GROUPS, SSD_STATE)
    dt = jax.nn.softplus(dt_raw.astype(f32) + dt_bias.astype(f32))
    a = -jnp.exp(a_log.astype(f32))
    state = ssm_h0.astype(f32)
    ys = []
    start = 0
    for seg_len, q in segments:
        sl = slice(start, start + seg_len)
        y_seg, state = ssd_segment(xs[:, sl], dt[:, sl], a, b_ssm[:, sl], c_ssm[:, sl], state, q)
        ys.append(y_seg)
        start += seg_len
    y = jnp.concatenate(ys, axis=1) + xs * d_skip.astype(f32)[:, None]
    y = y.reshape(bsz, l, SSD_INNER) * jax.nn.silu(z_ssd.astype(f32))
    yg = y.reshape(bsz, l, SSD_GROUPS, SSD_INNER // SSD_GROUPS)
    yg = yg * lax.rsqrt(jnp.mean(yg * yg, axis=-1, keepdims=True) + EPS)
    y = (yg.reshape(bsz, l, SSD_INNER) * ssd_norm_w.astype(f32)).astype(h.dtype)
    branch_ssd = y @ w_proj_ssd

    pool_ext = jnp.concatenate([pool_hist.astype(u.dtype), u], axis=1)
    means = pool_means(pool_ext, l, first_pos)
    pooled = (means - u).reshape(bsz, l, POOL_GROUPS, POOL_GROUP_WIDTH)
    mixed = jnp.einsum('blgc,gcd->blgd', pooled, pool_mix_w) + pool_mix_b
    p_out = mixed.reshape(bsz, l, POOL_WIDTH) * pool_scale * jax.nn.silu(z_pool)
    branch_pool = p_out @ w_proj_pool

    merged = gate_ssd * branch_ssd + gate_pool * branch_pool
    h_new = h + merged @ w_out
    pool_new = pool_ext[:, pool_ext.shape[1] - (POOL_MAX - 1):]
    return (h_new, conv_new.astype(conv_hist.dtype), state.astype(ssm_h0.dtype),
            pool_new.astype(pool_hist.dtype))


def setup_inputs(seed: int = 0) -> dict:
    key = jax.random.key(seed)
    ks = jax.random.split(key, 24)
    f32 = jnp.float32

    def nrm(k, shape, scale):
        return scale * jax.random.normal(k, shape, f32)

    dt0 = jnp.exp(jax.random.uniform(ks[10], (DEPTH, SSD_HEADS), f32, math.log(1e-3), math.log(1e-1)))
    return {
        "x_prompt": nrm(ks[0], (BATCH, SEQ, D_MODEL), 1.0),
        "x_sample": nrm(ks[1], (DEC_BATCH, DEC_SEQ, D_MODEL), 1.0),
        "state_conv": nrm(ks[2], (DEPTH, DEC_BATCH, SSD_CONV - 1, SSD_CONV_DIM), 1.0),
        "state_ssm": nrm(ks[3], (DEPTH, DEC_BATCH, SSD_HEADS, SSD_HEADDIM, SSD_STATE), 0.1),
        "state_pool": nrm(ks[4], (DEPTH, DEC_BATCH, POOL_MAX - 1, POOL_WIDTH), 1.0),
        "meta_tokens": nrm(ks[5], (N_META, D_MODEL), 1.0),
        "norm_w": 1.0 + nrm(ks[6], (DEPTH, D_MODEL), 0.02),
        "w_in": nrm(ks[7], (DEPTH, D_MODEL, D_IN_PROJ), D_MODEL ** -0.5),
        "conv_w": nrm(ks[8], (DEPTH, SSD_CONV, SSD_CONV_DIM), SSD_CONV ** -0.5),
        "conv_b": nrm(ks[9], (DEPTH, SSD_CONV_DIM), 0.01),
        "dt_bias": dt0 + jnp.log(-jnp.expm1(-dt0)),
        "a_log": jnp.log(jax.random.uniform(ks[11], (DEPTH, SSD_HEADS), f32, 1.0, 16.0)),
        "d_skip": 1.0 + nrm(ks[12], (DEPTH, SSD_HEADS), 0.02),
        "ssd_norm_w": 1.0 + nrm(ks[13], (DEPTH, SSD_INNER), 0.02),
        "w_proj_ssd": nrm(ks[14], (DEPTH, SSD_INNER, D_MODEL), SSD_INNER ** -0.5),
        "pool_mix_w": nrm(ks[15], (DEPTH, POOL_GROUPS, POOL_GROUP_WIDTH, POOL_GROUP_WIDTH), POOL_GROUP_WIDTH ** -0.5),
        "pool_mix_b": nrm(ks[16], (DEPTH, POOL_GROUPS, POOL_GROUP_WIDTH), 0.01),
        "pool_scale": 1.0 + nrm(ks[17], (DEPTH, POOL_WIDTH), 0.02),
        "w_proj_pool": nrm(ks[18], (DEPTH, POOL_WIDTH, D_MODEL), POOL_WIDTH ** -0.5),
        "w_out": nrm(ks[19], (DEPTH, D_MODEL, D_MODEL), D_MODEL ** -0.5),
        "final_norm_w": 1.0 + nrm(ks[20], (D_MODEL,), 0.02),
    }


def reference(x_prompt, x_sample, state_conv, state_ssm, state_pool, meta_tokens, norm_w, w_in,
              conv_w, conv_b, dt_bias, a_log, d_skip, ssd_norm_w, w_proj_ssd, pool_mix_w,
              pool_mix_b, pool_scale, w_proj_pool, w_out, final_norm_w):
    n_b, seq, _ = x_prompt.shape
    dec_seq = x_sample.shape[1]
    meta = jnp.broadcast_to(meta_tokens[None].astype(x_prompt.dtype), (n_b, N_META, D_MODEL))
    h_p = jnp.concatenate([meta, x_prompt], axis=1)
    h_s = x_sample
    seg_p = ((N_META, N_META), (seq, SSD_CHUNK))
    seg_s = ((dec_seq, dec_seq),)
    conv_p, ssm_p, pool_p, conv_s, ssm_s, pool_s = [], [], [], [], [], []
    for layer in range(DEPTH):
        lp = (norm_w[layer], w_in[layer], conv_w[layer], conv_b[layer], dt_bias[layer], a_log[layer],
              d_skip[layer], ssd_norm_w[layer], w_proj_ssd[layer], pool_mix_w[layer], pool_mix_b[layer],
              pool_scale[layer], w_proj_pool[layer], w_out[layer])
        h_p, c_new, s_new, q_new = mixer_layer(
            h_p,
            jnp.zeros((n_b, SSD_CONV - 1, SSD_CONV_DIM), state_conv.dtype),
            jnp.zeros((n_b, SSD_HEADS, SSD_HEADDIM, SSD_STATE), state_ssm.dtype),
            jnp.zeros((n_b, 0, POOL_WIDTH), state_pool.dtype),
            0, seg_p, lp)
        conv_p.append(c_new)
        ssm_p.append(s_new)
        pool_p.append(q_new)
        h_s, c_new, s_new, q_new = mixer_layer(
            h_s, state_conv[layer], state_ssm[layer], state_pool[layer], PAST_LEN, seg_s, lp)
        conv_s.append(c_new)
        ssm_s.append(s_new)
        pool_s.append(q_new)
    y_prompt = rms_norm(h_p, final_norm_w)[:, N_META:]
    y_sample = rms_norm(h_s, final_norm_w)
    new_conv_prompt = jnp.stack(conv_p)
    new_ssm_prompt = jnp.stack(ssm_p)
    new_pool_prompt = jnp.stack(pool_p)
    new_conv_sample = jnp.stack(conv_s)
    new_ssm_sample = jnp.stack(ssm_s)
    new_pool_sample = jnp.stack(pool_s)
    return (y_prompt, y_sample, new_conv_prompt, new_ssm_prompt, new_pool_prompt,
            new_conv_sample, new_ssm_sample, new_pool_sample)
```

```python
import numpy as np
from contextlib import ExitStack
import concourse.bass as bass
import concourse.mybir as mybir
from concourse.bass_utils import run_bass_kernel_spmd

F32 = mybir.dt.float32
BF16 = mybir.dt.bfloat16
AF = mybir.ActivationFunctionType
ALU = mybir.AluOpType

D = 2048
NH = 32
DIN = 13344
OFF_Z = 0
OFF_XBC = 2048
OFF_B = OFF_XBC + 2048
OFF_C = OFF_B + 512
OFF_DT = 5120
OFF_ZP = 5152
OFF_U = 7200
OFF_G1 = 9248
OFF_G2 = 11296
EPS = 1e-6
NCORES = 8
TM = 1168
TP = 1040
TF = 1152
WCOLS = 256
NWB = 3


class R:
    __slots__ = ("w", "r", "name", "excl")

    def __init__(self, name="", excl=False):
        self.w = None
        self.r = {}
        self.name = name
        self.excl = excl


class Sched:
    def __init__(self, nc, es):
        self.nc = nc
        self.eng = {"pe": nc.tensor, "act": nc.scalar, "dve": nc.vector, "pool": nc.gpsimd, "sp": nc.sync}
        self.semh = {}
        self.cnt = {}
        self.seen = {k: {} for k in self.eng}
        for k in self.eng:
            self.semh[k] = es.enter_context(nc.semaphore("s_" + k))
            self.cnt[k] = 0
        self.dslots = {}
        self.dnext = {}
        for q, n in (("sp", 12), ("pool", 8), ("act", 4)):
            self.dslots[q] = []
            for i in range(n):
                key = "d_%s%d" % (q, i)
                self.semh[key] = es.enter_context(nc.semaphore(key))
                self.cnt[key] = 0
                self.dslots[q].append(key)
            self.dnext[q] = 0
        self.nwaits = 0
        self.nops = 0

    def _deps(self, reads, writes, eng=None):
        deps = {}

        def add(k, v):
            if deps.get(k, 0) < v:
                deps[k] = v
        for r in reads:
            if r.w is not None:
                add(*r.w)
            if r.excl:
                for k, v in r.r.items():
                    if k != eng:
                        add(k, v)
        for w in writes:
            if w.w is not None:
                add(*w.w)
            for k, v in w.r.items():
                add(k, v)
        return deps

    def _wait(self, eng, deps):
        seen = self.seen[eng]
        for k, v in deps.items():
            if k == eng and eng in ("pe", "sp"):
                continue
            if seen.get(k, 0) < v:
                self.eng[eng].wait_ge(self.semh[k], v)
                seen[k] = v
                self.nwaits += 1

    def _mark(self, me, reads, writes):
        for w in writes:
            w.w = me
            w.r = {}
        ws = set(id(w) for w in writes)
        for r in reads:
            if id(r) not in ws:
                if r.r.get(me[0], 0) < me[1]:
                    r.r[me[0]] = me[1]

    def op(self, eng, fn, reads=(), writes=()):
        self._wait(eng, self._deps(reads, writes, eng))
        ins = fn(self.eng[eng])
        self.cnt[eng] += 1
        ins.then_inc(self.semh[eng], 1)
        self._mark((eng, self.cnt[eng]), reads, writes)
        self.nops += 1

    def dma(self, q, out, in_, reads=(), writes=()):
        deps = self._deps(reads, writes)
        slots = self.dslots[q]
        key = slots[self.dnext[q] % len(slots)]
        self.dnext[q] += 1
        if self.cnt[key] > 0:
            deps[key] = max(deps.get(key, 0), self.cnt[key])
        self._wait(q, deps)
        ins = self.eng[q].dma_start(out=out, in_=in_)
        self.cnt[key] += 16
        ins.then_inc(self.semh[key], 16)
        self._mark((key, self.cnt[key]), reads, writes)

    def barrier(self):
        for e in self.eng:
            deps = {k: v for k, v in self.cnt.items() if v > 0}
            self._wait(e, deps)

    def finish(self):
        self.barrier()


def _consts():
    c = np.zeros((128, 6 * 128 + 16), np.float32)
    i = np.arange(128)
    c[:, 0:128] = np.eye(128)
    c[:, 128:256] = (i[:, None] <= i[None, :])
    same = (i[:, None] // 8) == (i[None, :] // 8)
    c[:, 256:384] = (i[:, None] <= i[None, :]) & same
    c[:, 384:512] = same
    c[:, 512:640] = 1.0
    c[:, 640:704] = ((i[:, None] % 64) == np.arange(64)[None, :])
    c[:, 768:784] = (i[:, None] // 8) == np.arange(16)[None, :]
    c[:, 704:768] = (i[:, None] < 16)
    mT = ((np.arange(128)[None, :] // 8) == np.arange(16)[:, None]).astype(np.float32)
    return c, mT


def build_program(stop_after=None, dbg=()):
    nc = bass.Bass("TRN2", target_bir_lowering=False)
    dt_ = nc.dram_tensor

    def din(name, shape):
        return dt_(name, list(shape), F32, kind="ExternalInput").ap()

    def dout(name, shape):
        return dt_(name, list(shape), F32, kind="ExternalOutput").ap()

    xin = din("xin", [TM + TP, D])
    flags_d = din("flags", [128, 2])
    w_in = din("w_in", [D, DIN])
    w_ps = din("w_ps", [D, D])
    w_pp = din("w_pp", [D, D])
    w_out = din("w_out", [D, D])
    pmw = din("pmw", [4, 512, 512])
    normw_d = din("norm_w", [D])
    fnw_d = din("fnw", [D])
    ssdnw_d = din("ssdnw", [D])
    hvec_d = din("hvec", [3, NH])
    colp1_d = din("colp1", [120, 128])
    colp2_d = din("colp2", [32, 128])
    sconv_d = din("sconv", [48, 3072])
    spool_d = din("spool", [240, D])
    sssm_d = din("sssm", [16, D, 128])
    cst_d = din("cst", [128, 784])
    maskT_d = din("maskT", [16 * 128])

    y_d = dout("y", [TF, D])
    o_conv_p = dout("o_conv_p", [3, 3072])
    o_ssm_p = dout("o_ssm_p", [D, 128])
    o_pool_p = dout("o_pool_p", [15, D])
    o_conv_s = dout("o_conv_s", [48, 3072])
    o_ssm_s = dout("o_ssm_s", [16, D, 128])
    o_pool_s = dout("o_pool_s", [240, D])
    acscr = dt_("acscr", [10, NH * 128], F32, kind="Internal").ap()
    dbg_out = {}

    es = ExitStack()
    with es:
        S = Sched(nc, es)

        uid = [0]

        def sbt(stack, name, shape, dt):
            uid[0] += 1
            return stack.enter_context(nc.sbuf_tensor("sb%d_%s" % (uid[0], name), list(shape), dt))

        def dump(name, ap, shape, reads, dt=F32):
            if name in dbg:
                o = dt_("dbg_" + name, list(shape), dt, kind="ExternalOutput").ap()
                dbg_out[name] = o
                S.dma("sp", o, ap, reads=reads)

        PS = [es.enter_context(nc.psum_tensor("ps%d" % i, [128, 512], F32)) for i in range(8)]
        PSR = [R("ps%d" % i, excl=True) for i in range(8)]

        cst = sbt(es, "cst", [128, 784], F32)
        r_cst = R()
        S.dma("sp", cst[:], cst_d, writes=[r_cst])
        ident_f = cst[:, 0:128]
        tri = cst[:, 128:256]
        tri_b = cst[:, 256:384]
        ones_b = cst[:, 384:512]
        ones = cst[:, 512:640]
        i64 = cst[:, 640:704]
        seqm = cst[:, 768:784]
        ones16 = cst[:, 704:768]
        ident_bf = sbt(es, "ident_bf", [128, 128], BF16)
        r_idb = R()
        S.op("dve", lambda e: e.tensor_copy(ident_bf[:], ident_f), [r_cst], [r_idb])
        flags = sbt(es, "flags_s", [128, 2], F32)
        r_flags = R()
        S.dma("sp", flags[:], flags_d, writes=[r_flags])
        hv = sbt(es, "hv", [128, 4, NH], F32)
        r_hv = R()
        for j in range(3):
            S.dma("sp", hv[:, j, :], hvec_d[j].partition_broadcast(128), writes=[r_hv])
        S.op("act", lambda e: e.activation(out=hv[:, 3, :], in_=hv[:, 1, :], func=AF.Exp), [r_hv], [r_hv])
        S.op("dve", lambda e: e.tensor_scalar(out=hv[:, 3, :], in0=hv[:, 3, :], scalar1=-1.0, scalar2=None,
                                              op0=ALU.mult), [r_hv], [r_hv])
        dmat = sbt(es, "dmat", [128, NH, 64], BF16)
        r_dmat = R()
        S.op("dve", lambda e: e.tensor_tensor(out=dmat[:], in0=hv[:, 2, :].unsqueeze(2).to_broadcast([128, NH, 64]),
                                              in1=i64.unsqueeze(1).to_broadcast([128, NH, 64]), op=ALU.mult),
             [r_hv, r_cst], [r_dmat])
        cp = sbt(es, "cp", [128, 160], F32)
        r_cp = R()
        wb = [sbt(es, "wb%d" % i, [128, 16, WCOLS], BF16) for i in range(NWB)]
        stat = sbt(es, "stat", [128, 4, 24], F32)
        xy = sbt(es, "xy", [128, 16 * TM + 16 * TF], BF16)
        xnT_m = xy[:, 0:16 * TM].rearrange("p (k t) -> p k t", t=TM)
        yT = xy[:, 16 * TM:16 * TM + 16 * TF].rearrange("p (k t) -> p k t", t=TF)
        cp_stack = ExitStack()
        cp_tok = sbt(cp_stack, "cp_tok", [128, 2, 128], F32)
        r_cpt = R()
        S.dma("sp", cp_tok[0:120, 0, :], colp1_d, writes=[r_cpt])
        S.dma("sp", cp_tok[0:32, 1, :], colp2_d, writes=[r_cpt])

        def _tr_cp(e):
            e.transpose(PS[7][:, 0:120], cp_tok[0:120, 0, :], ident_f[0:120, 0:120])
            return e.transpose(PS[7][:, 128:160], cp_tok[0:32, 1, :], ident_f[0:32, 0:32])
        S.op("pe", _tr_cp, [r_cpt, r_cst], [PSR[7]])
        S.op("act", lambda e: e.copy(cp[:, 0:120], PS[7][:, 0:120]), [PSR[7]], [r_cp])
        S.op("act", lambda e: e.copy(cp[:, 120:152], PS[7][:, 128:160]), [PSR[7]], [r_cp])
        S.op("dve", lambda e: e.tensor_tensor(out=cp[:, 120:136], in0=cp[:, 120:136], in1=cp[:, 136:152], op=ALU.mult),
             [r_cp], [r_cp])
        S.barrier()
        cp_stack.close()
        r_fp = [R() for _ in range(4)]

        wbR = [[R() for _ in range(4)] for _ in range(NWB)]
        wsched = []
        wstate = {"issued": 0, "used": 0}

        def w_issue(upto):
            while wstate["issued"] < min(upto, len(wsched)):
                i = wstate["issued"]
                mat, c0, ncol, nk = wsched[i]
                slot = i % NWB
                src = mat.rearrange("(k p) e -> p k e", p=128)
                nq = 4 if nk == 16 else 1
                kq = nk // nq
                for q in range(nq):
                    S.dma("pool", wb[slot][:, q * kq:(q + 1) * kq, 0:ncol], src[:, q * kq:(q + 1) * kq, c0:c0 + ncol],
                          writes=[wbR[slot][q]])
                wstate["issued"] += 1

        def w_next(keep=0):
            i = wstate["used"]
            w_issue(i + NWB - keep)
            wstate["used"] += 1
            slot = i % NWB
            nk = wsched[i][3]
            return wb[slot], (wbR[slot] if nk == 16 else wbR[slot][0:1])

        def w_add(mat, c0, ncol, nk=16):
            wsched.append((mat, c0, ncol, nk))

        w_add(w_in, OFF_DT, 32)
        for g in range(4):
            w_add(w_in, OFF_XBC + 512 * g, 256)
            w_add(w_in, OFF_XBC + 512 * g + 256, 256)
            w_add(w_in, OFF_B + 128 * g, 128)
        w_add(w_in, OFF_DT, 32)
        for g in range(4):
            w_add(w_in, OFF_XBC + 512 * g, 256)
            w_add(w_in, OFF_XBC + 512 * g + 256, 256)
            w_add(w_in, OFF_B + 128 * g, 128)
            w_add(w_in, OFF_C + 128 * g, 128)
            w_add(w_in, OFF_Z + 512 * g, 256)
            w_add(w_in, OFF_Z + 512 * g + 256, 256)
        for g in range(4):
            w_add(w_in, OFF_U + 512 * g, 256)
            w_add(w_in, OFF_U + 512 * g + 256, 256)
            w_add(w_in, OFF_ZP + 512 * g, 256)
            w_add(w_in, OFF_ZP + 512 * g + 256, 256)
            w_add(pmw[g], 0, 256, 4)
            w_add(pmw[g], 256, 256, 4)
        for mb in range(8):
            w_add(w_ps, 256 * mb, 256)
            w_add(w_in, OFF_G1 + 256 * mb, 256)
            w_add(w_pp, 256 * mb, 256)
            w_add(w_in, OFF_G2 + 256 * mb, 256)
        for db in range(8):
            w_add(w_out, 256 * db, 256)

        r_stat = R()
        S.op("dve", lambda e: e.memset(stat[:], 0.0), [], [r_stat])

        def phase_a(stack, xnT, xnT_R, tiles, row_base, slot0):
            sub = ExitStack()
            with sub:
                vecbc = sbt(sub, "normw_bc", [128, D], F32)
                r_vec = R()
                S.dma("sp", vecbc[:], normw_d.partition_broadcast(128), writes=[r_vec])
                NXB = 3
                xt = [sbt(sub, "xt%d" % i, [128, D], F32) for i in range(NXB)]
                xtR = [R() for _ in range(NXB)]
                xnb = [sbt(sub, "xnb%d" % i, [128, D], BF16) for i in range(NXB)]
                xnbR = [R() for _ in range(NXB)]
                junk = sbt(sub, "junk_a", [128, D], BF16)
                r_junk = R()
                nt = len(tiles)
                stR = [R() for _ in range(nt)]
                for r_ in stR:
                    r_.w = r_stat.w

                def stats1(ti):
                    rows, col0 = tiles[ti]
                    b = ti % NXB
                    sl = slot0 + ti
                    S.dma("sp", xt[b][0:rows, :], xin[row_base + col0:row_base + col0 + rows, :], writes=[xtR[b]])
                    S.op("act", lambda e: e.activation(out=junk[0:rows, :], in_=xt[b][0:rows, :], func=AF.Square,
                                                       accum_out=stat[0:rows, 0, sl:sl + 1]),
                         [xtR[b]], [r_junk, stR[ti]])

                def stats2(ti):
                    rows, col0 = tiles[ti]
                    sl = slot0 + ti
                    S.op("dve", lambda e: e.tensor_scalar(out=stat[0:rows, 1, sl:sl + 1], in0=stat[0:rows, 0, sl:sl + 1],
                                                          scalar1=1.0 / D, scalar2=EPS, op0=ALU.mult, op1=ALU.add),
                         [stR[ti]], [stR[ti]])
                    S.op("act", lambda e: e.activation(out=stat[0:rows, 2, sl:sl + 1], in_=stat[0:rows, 1, sl:sl + 1],
                                                       func=AF.Sqrt), [stR[ti]], [stR[ti]])
                    S.op("dve", lambda e: e.reciprocal(stat[0:rows, 3, sl:sl + 1], stat[0:rows, 2, sl:sl + 1]),
                         [stR[ti]], [stR[ti]])
                stats1(0)
                stats2(0)
                for ti, (rows, col0) in enumerate(tiles):
                    b = ti % NXB
                    sl = slot0 + ti
                    if ti + 1 < nt:
                        stats1(ti + 1)
                    S.op("dve", lambda e: e.scalar_tensor_tensor(out=xnb[b][0:rows, :], in0=xt[b][0:rows, :],
                                                                 scalar=stat[0:rows, 3, sl:sl + 1], in1=vecbc[0:rows, :],
                                                                 op0=ALU.mult, op1=ALU.mult),
                         [xtR[b], stR[ti], r_vec], [xnbR[b]])
                    pbs_ = []
                    for half in range(2):
                        pb = (0, 1, 2, 3, 5, 6)[(2 * ti + half) % 6]
                        pbs_.append(pb)
                        psb = PS[pb][:].bitcast(BF16)

                        def _tr(e, half=half, psb=psb):
                            ins = None
                            for kk in range(8):
                                k = half * 8 + kk
                                ins = e.transpose(psb[:, kk * 128:kk * 128 + rows], xnb[b][0:rows, k * 128:(k + 1) * 128],
                                                  ident_bf[0:rows, 0:rows])
                            return ins
                        S.op("pe", _tr, [xnbR[b], r_idb], [PSR[pb]])
                    if ti + 1 < nt:
                        stats2(ti + 1)
                    for half in range(2):
                        pb = pbs_[half]
                        psb = PS[pb][:].bitcast(BF16)
                        src = psb.rearrange("p (k t) -> p k t", t=128)[:, :, 0:rows]
                        dst = xnT[:, half * 8:(half + 1) * 8, col0:col0 + rows]
                        if half == 0:
                            S.op("act", lambda e: e.copy(dst, src), [PSR[pb]], [xnT_R[ti]])
                        else:
                            S.op("dve", lambda e: e.tensor_copy(dst, src), [PSR[pb]], [xnT_R[ti]])
                S.barrier()

        tiles_p = [(16, 0)] + [(128, 16 + 128 * i) for i in range(8)]
        tiles_m = [(16, 0)] + [(128, 16 + 128 * i) for i in range(8)] + [(128, 1040)]

        xnT_mR = [R() for _ in tiles_m]
        phase_a(es, xnT_m, xnT_mR, tiles_m, 0, 0)
        dump("xnT_m", xnT_m, [128, 16, TM], xnT_mR, BF16)

        if stop_after == "A":
            S.finish()
            return nc, dbg_out

        yTR = [[R() for _ in range(9)] for _ in range(4)]

        def ssd_segment(kind, xnT, xnTR, tiles):
            main = kind == "main"
            nt = len(tiles)
            ncols = TM if main else TP
            tblocks = [(0, 512), (512, 1024), (1024, ncols)]
            sub = ExitStack()
            with sub:
                acc_pre = sbt(sub, "acc", [128, 1040], F32)
                r_acc_pre = R()
                dtb = sbt(sub, "dtb", [128, 6, nt, NH], F32)
                r_dtb = R()
                DTA, DT, BNEG, AC, CDB, WGT = range(6)
                X, LNDT, TOT, EA = DTA, BNEG, CDB, DTA
                wt, wr = w_next()

                def _dtmm(e):
                    ins = None
                    for ti, (rows, col0) in enumerate(tiles):
                        for k in range(16):
                            ins = e.matmul(PS[4][:, ti * NH:(ti + 1) * NH], lhsT=xnT[:, k, col0:col0 + 128],
                                           rhs=wt[:, k, 0:NH], start=(k == 0), stop=(k == 15))
                    return ins
                S.op("pe", _dtmm, list(xnTR) + list(wr), [PSR[4]])
                psdt = PS[4][:, 0:nt * NH].rearrange("p (t h) -> p t h", h=NH)

                def bc_h(j):
                    return hv[:, j, :].unsqueeze(1).to_broadcast([128, nt, NH])
                S.op("dve", lambda e: e.tensor_tensor(out=dtb[:, X], in0=psdt, in1=bc_h(0), op=ALU.add),
                     [PSR[4], r_hv], [r_dtb])
                S.op("dve", lambda e: e.tensor_scalar(out=dtb[:, WGT], in0=dtb[:, X], scalar1=0.0, scalar2=None, op0=ALU.max),
                     [r_dtb], [r_dtb])
                S.op("dve", lambda e: e.scalar_tensor_tensor(out=dtb[:, DT], in0=dtb[:, WGT], scalar=-2.0, in1=dtb[:, X],
                                                             op0=ALU.mult, op1=ALU.add), [r_dtb], [r_dtb])
                S.op("act", lambda e: e.activation(out=dtb[:, DT], in_=dtb[:, DT], func=AF.Exp), [r_dtb], [r_dtb])
                S.op("act", lambda e: e.activation(out=dtb[:, DT], in_=dtb[:, DT], func=AF.Ln, bias=1.0),
                     [r_dtb], [r_dtb])
                S.op("dve", lambda e: e.tensor_tensor(out=dtb[:, DT], in0=dtb[:, DT], in1=dtb[:, WGT], op=ALU.add),
                     [r_dtb], [r_dtb])
                S.op("dve", lambda e: e.tensor_tensor(out=dtb[:, DTA], in0=dtb[:, DT], in1=bc_h(3), op=ALU.mult),
                     [r_dtb, r_hv], [r_dtb])
                S.op("dve", lambda e: e.tensor_scalar(out=dtb[:, DTA, 0, :], in0=dtb[:, DTA, 0, :], scalar1=ones16[:, 0:1],
                                                      scalar2=None, op0=ALU.mult), [r_dtb, r_cst], [r_dtb])
                S.op("act", lambda e: e.activation(out=dtb[:, LNDT], in_=dtb[:, DT], func=AF.Ln, bias=1e-30), [r_dtb], [r_dtb])

                r_acs = [R() for _ in range(nt)]

                def pre2():
                    def _cum(e):
                        ins = None
                        for ti, (rows, col0) in enumerate(tiles):
                            smp = main and ti == nt - 1
                            tr = tri_b if smp else tri
                            on = ones_b if smp else ones
                            e.matmul(PS[4][:, ti * NH:(ti + 1) * NH], lhsT=tr, rhs=dtb[:, DTA, ti, :], start=True, stop=True)
                            ins = e.matmul(PS[7][:, ti * NH:(ti + 1) * NH], lhsT=on, rhs=dtb[:, DTA, ti, :], start=True, stop=True)
                        return ins
                    S.op("pe", _cum, [r_dtb, r_cst], [PSR[4], PSR[7]])
                    S.op("act", lambda e: e.copy(dtb[:, AC], PS[4][:, 0:nt * NH].rearrange("p (t h) -> p t h", h=NH)),
                         [PSR[4]], [r_dtb])
                    S.op("act", lambda e: e.copy(dtb[:, TOT], PS[7][:, 0:nt * NH].rearrange("p (t h) -> p t h", h=NH)),
                         [PSR[7]], [r_dtb])
                    S.op("dve", lambda e: e.tensor_tensor(out=dtb[:, BNEG], in0=dtb[:, LNDT], in1=dtb[:, AC], op=ALU.subtract),
                         [r_dtb], [r_dtb])
                    S.op("dve", lambda e: e.tensor_tensor(out=dtb[:, WGT], in0=dtb[:, TOT], in1=dtb[:, AC], op=ALU.subtract),
                         [r_dtb], [r_dtb])
                    S.op("act", lambda e: e.activation(out=dtb[:, WGT], in_=dtb[:, WGT], func=AF.Exp), [r_dtb], [r_dtb])
                    S.op("dve", lambda e: e.tensor_tensor(out=dtb[:, WGT], in0=dtb[:, WGT], in1=dtb[:, DT], op=ALU.mult),
                         [r_dtb], [r_dtb])
                    S.op("act", lambda e: e.activation(out=dtb[:, CDB], in_=dtb[:, TOT], func=AF.Exp), [r_dtb], [r_dtb])
                    if not main:
                        for ti in range(nt - 2, -1, -1):
                            src_w = dtb[:, CDB, ti + 1, :]
                            if ti < nt - 2:
                                S.op("dve", lambda e: e.tensor_tensor(out=dtb[:, BNEG, ti, :], in0=dtb[:, BNEG, ti + 1, :], in1=src_w,
                                                                      op=ALU.mult), [r_dtb], [r_dtb])
                            else:
                                S.op("dve", lambda e: e.tensor_copy(dtb[:, BNEG, ti, :], src_w), [r_dtb], [r_dtb])
                            S.op("dve", lambda e: e.tensor_tensor(out=dtb[:, WGT, ti, :], in0=dtb[:, WGT, ti, :], in1=dtb[:, BNEG, ti, :],
                                                                  op=ALU.mult), [r_dtb], [r_dtb])
                    if main:
                        acT = acc_pre[0:NH, 0:384].rearrange("p (i t) -> p i t", t=128)
                        r_acT = r_acc_pre
                        for t0 in range(1, nt, 3):
                            tl = list(range(t0, min(t0 + 3, nt)))
                            pb = 5

                            def _acT(e):
                                ins = None
                                for ii, ti in enumerate(tl):
                                    smp = ti == nt - 1
                                    ins = e.matmul(PS[pb][0:NH, ii * 128:(ii + 1) * 128], lhsT=dtb[:, DTA, ti, :],
                                                   rhs=(tri_b if smp else tri), start=True, stop=True)
                                return ins
                            S.op("pe", _acT, [r_dtb, r_cst], [PSR[pb]])
                            S.op("act", lambda e: e.copy(acT[:, 0:len(tl), :],
                                                         PS[pb][0:NH, 0:len(tl) * 128].rearrange("p (i t) -> p i t", t=128)),
                                 [PSR[pb]], [r_acT])
                            for ii, ti in enumerate(tl):
                                S.dma("sp", acscr[ti].rearrange("(h t) -> h t", t=128), acT[:, ii, :], reads=[r_acT],
                                      writes=[r_acs[ti]])
                    if main:
                        S.op("act", lambda e: e.activation(out=dtb[:, EA], in_=dtb[:, AC], func=AF.Exp), [r_dtb], [r_dtb])
                    dump("dtb_" + kind, dtb[:], [128, 6, nt, NH], [r_dtb])

                if main:
                    extm = [sbt(sub, "extm", [128, 3 + 1040], F32)] * 2
                    extmR = [R()] * 2
                else:
                    extm = [sbt(sub, "extm%d" % i, [128, 3 + 1040], F32) for i in range(2)]
                    extmR = [R(), R()]
                acc = acc_pre
                r_acc = r_acc_pre
                for i in range(2):
                    S.op("dve", lambda e: e.memset(extm[i][:, 0:3], 0.0), [], [extmR[i]])
                nch = 6 if main else 5
                xbcT = sbt(sub, "xbcT", [128, nch, ncols], BF16)
                xbcTR = [R() for _ in range(nch)]
                xsb = sbt(sub, "xsb", [128, nt, 640], BF16)
                xsbR = [R() for _ in range(nt)]
                hst = sbt(sub, "hst", [128, 8, 64], F32)
                r_h = R()
                hbf = sbt(sub, "hbf", [128, 512], BF16)
                r_hb = R()
                t1 = sbt(sub, "t1", [128, 8, 64], F32)
                r_t1 = R()
                xdd = sbt(sub, "xdd", [128, 8, 64], BF16)
                r_xdd = R()
                if main:
                    exts = [sbt(sub, "exts", [128, 16, 11], F32)] * 2
                    extsR = [R()] * 2
                    accs = sbt(sub, "accs", [128, 16, 8], F32)
                    r_accs = R()
                    pre_s = sbt(sub, "pre_s", [128, 131], F32)
                    r_pre = R()
                    pcs_t = sbt(sub, "pcs", [128, 6, 128], F32)
                    pcs = pcs_t[0:48]
                    pcp = pcs_t[64:67]
                    r_pcs = R()
                    r_pcp = R()
                    zs = sbt(sub, "zs", [128, 9, 512], BF16)
                    zsR = [R() for _ in range(9)]
                    acbc = [sbt(sub, "acbc%d" % i, [128, 8, 128], F32) for i in range(2)]
                    acbcR = [R(), R()]
                    Mq = sbt(sub, "Mq", [128, 8, 128], BF16)
                    cbm = sbt(sub, "cbm", [128, 128], F32)
                    Mq2 = [Mq[:], pcs_t[:].rearrange("p a b -> p (a b)")[:, 0:512].bitcast(BF16).rearrange("p (h t) -> p h t", t=128)]
                    MqR = [R(), r_pcs]
                    cbm2 = [cbm[:], pre_s[:, 0:128]]
                    cbmR = [R(), r_pre]
                    yv = sbt(sub, "yv", [128, 8, 64], F32)
                    r_yv = R()
                    yn = sbt(sub, "yn", [128, 512], BF16)
                    junk = sbt(sub, "junk_m", [128, 512], BF16)
                    r_junk = R()
                    zth = extm[0][:, 272:528].bitcast(BF16)
                    r_zth = R()
                    cdx_s = extm[0][:, 528:1040].rearrange("p (h q) -> p h q", q=64)
                    xdd_s = extm[0][:, 528:784].bitcast(BF16).rearrange("p (h q) -> p h q", q=64)
                    r_sx = R()
                    negh = sbt(sub, "negh", [128, 1], F32)
                    r_negh = R()
                    S.op("dve", lambda e: e.memset(negh[:], -0.5), [], [r_negh])
                    yn2 = [yn[:], extm[0][:, 16:272].bitcast(BF16)]
                    ynR = [R(), extmR[0]]
                    gst = sbt(sub, "gst", [128, 4, 40], F32)
                    r_gst = R()
                    S.op("dve", lambda e: e.memset(gst[:], 0.0), [], [r_gst])
                    nwbc = sbt(sub, "nwbc", [128, 512], F32)
                    r_nw = R()
                    sto = yv
                    r_sto = r_yv
                    scT = sbt(sub, "scT", [128, 6, 48], F32)
                    r_scT = R()
                    ctm = sbt(sub, "ctm", [128, 16, 128], BF16)
                    r_ctm = R()
                    bmk = acc[:, 0:1024].bitcast(BF16).rearrange("p (j n) -> p j n", n=128)
                    r_bmk = r_acc
                    h0n_t = [sbt(sub, "h0n%d" % i, [128, 4, 128], F32) for i in range(2)]
                    h0n = [h0n_t[0][:], h0n_t[1][:], acbc[0][:, 0:4, :], acbc[0][:, 4:8, :], acbc[1][:, 0:4, :], acbc[1][:, 4:8, :]]
                    h0nR = [R() for _ in range(6)]
                    h0nX = [[], [], [acbcR[0]], [acbcR[0]], [acbcR[1]], [acbcR[1]]]
                    h0T = [sbt(sub, "h0T%d" % i, [128, 512], BF16) for i in range(2)]
                    h0TR = [R(), R()]
                    cdx = t1
                    r_cdx = r_t1
                    cdcol = sbt(sub, "cdcol", [128, 4, 16], F32)
                    r_cdc = R()

                psrot = [0]

                def next_ps():
                    b = psrot[0] % 3
                    psrot[0] += 1
                    return b

                for g in range(4):
                    if main:
                        S.dma("sp", nwbc[:], ssdnw_d[512 * g:512 * g + 512].partition_broadcast(128), writes=[r_nw])
                        for jj, (cc0, ncc) in enumerate(((512 * g, 512), (2048 + 128 * g, 128), (2560 + 128 * g, 128))):
                            o0 = 0 if jj == 0 else (512 if jj == 1 else 640)
                            S.dma("sp", acc[0:48, o0:o0 + ncc], sconv_d[:, cc0:cc0 + ncc], writes=[r_acc])

                        def _trs(e):
                            ins = None
                            for ii in range(6):
                                ins = e.transpose(PS[6][:, ii * 48:(ii + 1) * 48], acc[0:48, ii * 128:(ii + 1) * 128],
                                                  ident_f[0:48, 0:48])
                            return ins
                        S.op("pe", _trs, [r_acc, r_cst], [PSR[6]])
                        S.op("act", lambda e: e.copy(scT[:], PS[6][:, 0:288].rearrange("p (i t) -> p i t", t=48)),
                             [PSR[6]], [r_scT])
                    chunk_list = [(0, 0), (0, 1), (1, 0), (1, 1), (2, 0)] + ([(3, 0)] if main else [])
                    wcur = {}
                    for j, (wi, sub_j) in enumerate(chunk_list):
                        if wi not in wcur:
                            wcur.clear()
                            wcur[wi] = w_next()
                        wt, wr = wcur[wi]
                        if j < 4:
                            cglob = 4 * g + j
                        elif j == 4:
                            cglob = 16 + g
                        else:
                            cglob = 20 + g
                        eb = j % 2
                        for bi, (c0, c1) in enumerate(tblocks):
                            n = c1 - c0
                            pb = next_ps()

                            def _mm(e):
                                ins = None
                                for k in range(16):
                                    ins = e.matmul(PS[pb][:, 0:n], lhsT=wt[:, k, sub_j * 128:(sub_j + 1) * 128],
                                                   rhs=xnT[:, k, c0:c1], start=(k == 0), stop=(k == 15))
                                return ins
                            S.op("pe", _mm, list(xnTR) + list(wr), [PSR[pb]])
                            if bi < 2:
                                S.op("act", lambda e: e.copy(extm[eb][:, 3 + c0:3 + c1], PS[pb][:, 0:n]),
                                     [PSR[pb]], [extmR[eb]] + ([r_zth, r_sx] if main else []))
                            else:
                                S.op("act", lambda e: e.copy(extm[eb][:, 3 + 1024:3 + 1040], PS[pb][:, 0:16]),
                                     [PSR[pb]], [extmR[eb]])
                                if main:
                                    S.op("act", lambda e: e.copy(pre_s[:, 0:128].rearrange("p (i j) -> p i j", j=16),
                                                                 PS[pb][:, 16:144].rearrange("p (j i) -> p i j", i=8)), [PSR[pb]], [r_pre])
                                    S.op("act", lambda e: e.copy(pre_s[:, 128:131], PS[pb][:, 13:16]), [PSR[pb]], [r_pre])
                                    S.op("dve", lambda e: e.tensor_copy(exts[eb][:, :, 3:11],
                                                                         PS[pb][:, 16:144].rearrange("p (j i) -> p j i", i=8)),
                                         [PSR[pb]], [extsR[eb]])
                                    S.op("dve", lambda e: e.tensor_copy(exts[eb][:, :, 0:3],
                                                                         scT[:, j, :].rearrange("p (j r) -> p j r", r=3)),
                                         [r_scT], [extsR[eb]])
                        L = 1040

                        def cw(k):
                            return cp[:, k * 24 + cglob:k * 24 + cglob + 1]
                        S.op("dve", lambda e: e.tensor_scalar(out=acc[:, 0:L], in0=extm[eb][:, 0:L], scalar1=cw(0), scalar2=None,
                                                              op0=ALU.mult), [extmR[eb], r_cp], [r_acc])
                        for k in range(1, 4):
                            S.op("dve", lambda e: e.scalar_tensor_tensor(out=acc[:, 0:L], in0=extm[eb][:, k:k + L], scalar=cw(k),
                                                                         in1=acc[:, 0:L], op0=ALU.mult, op1=ALU.add),
                                 [extmR[eb], r_cp, r_acc], [r_acc])
                        S.op("act", lambda e: e.activation(out=xbcT[:, j, 0:L], in_=acc[:, 0:L], func=AF.Silu,
                                                           bias=cp[:, 96 + cglob:97 + cglob]), [r_acc, r_cp], [xbcTR[j]])
                        if main:
                            S.op("dve", lambda e: e.tensor_scalar(out=accs[:], in0=exts[eb][:, :, 0:8], scalar1=cw(0), scalar2=None,
                                                                  op0=ALU.mult), [extsR[eb], r_cp], [r_accs])
                            for k in range(1, 4):
                                S.op("dve", lambda e: e.scalar_tensor_tensor(out=accs[:], in0=exts[eb][:, :, k:k + 8], scalar=cw(k),
                                                                             in1=accs[:], op0=ALU.mult, op1=ALU.add),
                                     [extsR[eb], r_cp, r_accs], [r_accs])
                            S.op("act", lambda e: e.activation(out=xbcT[:, j, L:L + 128].rearrange("p (j i) -> p j i", i=8),
                                                               in_=accs[:], func=AF.Silu, bias=cp[:, 96 + cglob:97 + cglob]),
                                 [r_accs, r_cp], [xbcTR[j]])
                            S.op("pe", lambda e: e.transpose(PS[7][0:51, 0:128], pre_s[:, 80:131], ident_f), [r_pre, r_cst], [PSR[7]])
                            S.op("act", lambda e: e.copy(pcs_t[0:51, j, :], PS[7][0:51, 0:128]), [PSR[7]], [r_pcs])
                    if main:
                        ocs = o_conv_s.rearrange("(j r) c -> j r c", r=3)
                        for (cc0, ncc, j0, j1) in ((512 * g, 512, 0, 4), (2048 + 128 * g, 128, 4, 5), (2560 + 128 * g, 128, 5, 6)):
                            for r3 in range(3):
                                S.dma("sp", ocs[:, r3, cc0:cc0 + ncc].rearrange("j (c q) -> j c q", q=128),
                                      pcs_t[16 * r3:16 * r3 + 16, j0:j1, :], reads=[r_pcs])
                            S.dma("sp", o_conv_p[:, cc0:cc0 + ncc].rearrange("r (c q) -> r c q", q=128), pcs_t[48:51, j0:j1, :],
                                  reads=[r_pcs])
                    if g == 0:
                        pre2()
                        dump("xbcT_" + kind, xbcT[:], [128, nch, ncols], xbcTR, BF16)

                    psb3 = PS[3][:].bitcast(BF16)
                    for ti, (rows, col0) in enumerate(tiles):
                        pbx = (3, 7)[ti % 2]
                        psbx = PS[pbx][:].bitcast(BF16)

                        def _trx(e):
                            ins = None
                            for jj in range(5):
                                ins = e.transpose(psbx[0:rows, jj * 128:(jj + 1) * 128], xbcT[:, jj, col0:col0 + rows], ident_bf[:])
                            return ins
                        S.op("pe", _trx, xbcTR[0:5] + [r_idb], [PSR[pbx]])
                        if ti % 2 == 0:
                            S.op("act", lambda e: e.copy(xsb[0:rows, ti, :], psbx[0:rows, 0:640]), [PSR[pbx]], [xsbR[ti]])
                        else:
                            S.op("dve", lambda e: e.tensor_copy(xsb[0:rows, ti, :], psbx[0:rows, 0:640]), [PSR[pbx]], [xsbR[ti]])

                    if main:
                        wz = [w_next(), w_next(keep=1)]

                        def zproj(ti):
                            rows, col0 = tiles[ti]
                            for half in range(2):
                                wt, wr = wz[half]
                                pb = next_ps()

                                def _zmm(e):
                                    ins = None
                                    for k in range(16):
                                        ins = e.matmul(PS[pb][:, 0:256], lhsT=xnT[:, k, col0:col0 + 128], rhs=wt[:, k, 0:256],
                                                       start=(k == 0), stop=(k == 15))
                                    return ins
                                S.op("pe", _zmm, [xnTR[ti]] + list(wr), [PSR[pb]])
                                S.op("act", lambda e: e.activation(out=zth[:, half * 256:(half + 1) * 256], in_=PS[pb][:, 0:256],
                                                                   func=AF.Tanh, scale=0.5), [PSR[pb]], [r_zth])
                                S.op("dve", lambda e: e.scalar_tensor_tensor(out=zs[:, ti - 1, half * 256:(half + 1) * 256],
                                                                             in0=zth[:, half * 256:(half + 1) * 256], scalar=1.0,
                                                                             in1=PS[pb][:, 0:256], op0=ALU.add, op1=ALU.mult),
                                     [r_zth, PSR[pb]], [zsR[ti - 1]])

                    hsl = slice(8 * g, 8 * g + 8)
                    xddf = xdd[:].rearrange("p h q -> p (h q)")
                    hf_ = hst[:].rearrange("p h q -> p (h q)")
                    t1f = t1[:].rearrange("p h q -> p (h q)")

                    def mprep(ti):
                        rows, col0 = tiles[ti]
                        smp = ti == nt - 1
                        bb = ti % 2
                        S.dma("sp", acbc[bb][:].rearrange("p h t -> p (h t)"),
                              acscr[ti, g * 1024:(g + 1) * 1024].partition_broadcast(128), reads=[r_acs[ti]],
                              writes=[acbcR[bb], h0nR[2 + 2 * bb], h0nR[3 + 2 * bb]])
                        S.op("dve", lambda e: e.tensor_tensor(out=acbc[bb][:], in0=acbc[bb][:],
                                                              in1=dtb[:, AC, ti, hsl].unsqueeze(2).to_broadcast([128, 8, 128]),
                                                              op=ALU.min), [acbcR[bb], r_dtb], [acbcR[bb]])
                        S.op("pool", lambda e: e.tensor_tensor(out=acbc[bb][:], in0=acbc[bb][:],
                                                               in1=dtb[:, BNEG, ti, hsl].unsqueeze(2).to_broadcast([128, 8, 128]),
                                                               op=ALU.add), [acbcR[bb], r_dtb], [acbcR[bb]])
                        S.op("act", lambda e: e.activation(out=acbc[bb][:], in_=acbc[bb][:], func=AF.Exp), [acbcR[bb]], [acbcR[bb]])
                        S.op("pe", lambda e: e.matmul(PS[4][:, 0:128], lhsT=xbcT[:, 4, col0:col0 + 128],
                                                      rhs=xbcT[:, 5, col0:col0 + 128], start=True, stop=True),
                             [xbcTR[4], xbcTR[5]], [PSR[4]])

                    def mprep2(ti):
                        rows, col0 = tiles[ti]
                        smp = ti == nt - 1
                        bb = ti % 2
                        S.op("dve", lambda e: e.tensor_tensor(out=cbm2[bb], in0=PS[4][:, 0:128], in1=(tri_b if smp else tri),
                                                              op=ALU.mult), [PSR[4], r_cst], [cbmR[bb]])
                        S.op("dve", lambda e: e.tensor_tensor(out=Mq2[bb], in0=acbc[bb][:],
                                                              in1=cbm2[bb].unsqueeze(1).to_broadcast([128, 8, 128]),
                                                              op=ALU.mult), [acbcR[bb], cbmR[bb]], [MqR[bb]])

                    def ymm(ti):
                        rows, col0 = tiles[ti]
                        bb = ti % 2

                        def _ymm(e):
                            ins = None
                            for h in range(8):
                                qb = 64 * (h % 2)
                                e.matmul(PS[5][:, h * 64:(h + 1) * 64], lhsT=Mq2[bb][:, h, :], rhs=xsb[:, ti, h * 64:(h + 1) * 64],
                                         start=True, stop=False)
                                ins = e.matmul(PS[5][:, h * 64:(h + 1) * 64], lhsT=xbcT[qb:qb + 64, h // 2, col0:col0 + 128],
                                               rhs=dmat[qb:qb + 64, 8 * g + h, :], start=False, stop=True)
                            return ins
                        S.op("pe", _ymm, [MqR[bb], xsbR[ti], r_dmat] + xbcTR[0:4], [PSR[5]])

                    def post(ti):
                        S.op("dve", lambda e: e.tensor_tensor(out=yv[:], in0=PS[6][:, 0:512].rearrange("p (h q) -> p h q", q=64),
                                                              in1=dtb[:, EA, ti, hsl].unsqueeze(2).to_broadcast([128, 8, 64]),
                                                              op=ALU.mult), [PSR[6], r_dtb], [r_yv])
                        yvf = yv[:].rearrange("p h q -> p (h q)")
                        S.op("dve", lambda e: e.tensor_tensor(out=yvf, in0=yvf, in1=PS[5][:, 0:512], op=ALU.add),
                             [r_yv, PSR[5]], [r_yv])
                        S.op("dve", lambda e: e.tensor_tensor(out=yvf, in0=yvf, in1=zs[:, ti - 1, :], op=ALU.mult),
                             [r_yv, zsR[ti - 1]], [r_yv])
                        sl = g * 10 + ti
                        S.op("act", lambda e: e.activation(out=junk[:], in_=yvf, func=AF.Square,
                                                           accum_out=gst[:, 0, sl:sl + 1]), [r_yv], [r_junk, r_gst])

                    def post2(ti):
                        yvf = yv[:].rearrange("p h q -> p (h q)")
                        sl = g * 10 + ti
                        S.op("dve", lambda e: e.tensor_scalar(out=gst[:, 1, sl:sl + 1], in0=gst[:, 0, sl:sl + 1],
                                                              scalar1=1.0 / 512, scalar2=4.0 * EPS, op0=ALU.mult, op1=ALU.add),
                             [r_gst], [r_gst])
                        S.op("pool", lambda e: e.tensor_tensor(out=gst[:, 3, sl:sl + 1], in0=gst[:, 1, sl:sl + 1], in1=negh[:],
                                                               op=ALU.pow), [r_gst, r_negh], [r_gst])
                        yb = ti % 2
                        S.op("dve", lambda e: e.scalar_tensor_tensor(out=yn2[yb], in0=yvf, scalar=gst[:, 3, sl:sl + 1],
                                                                     in1=nwbc[:], op0=ALU.mult,
                                                                     op1=ALU.mult), [r_yv, r_gst, r_nw], [ynR[yb]])

                    def post_b(ti):
                        yb = ti % 2

                        def _try(e):
                            ins = None
                            for c in range(4):
                                ins = e.transpose(psb3[:, c * 128:(c + 1) * 128], yn2[yb][:, c * 128:(c + 1) * 128], ident_bf[:])
                            return ins
                        S.op("pe", _try, [ynR[yb], r_idb], [PSR[3]])
                        tc0 = (ti - 1) * 128
                        S.op("act", lambda e: e.copy(yT[:, 4 * g:4 * g + 4, tc0:tc0 + 128],
                                                     psb3[:, 0:512].rearrange("p (c t) -> p c t", t=128)), [PSR[3]],
                             [yTR[g][ti - 1]])

                    def mk_xdd(ti):
                        rows, col0 = tiles[ti]
                        S.op("pool", lambda e: e.tensor_tensor(out=xdd[0:rows], in0=xsb[0:rows, ti, 0:512].rearrange("p (h q) -> p h q", q=64),
                                                               in1=dtb[0:rows, WGT, ti, hsl].unsqueeze(2).to_broadcast([rows, 8, 64]),
                                                               op=ALU.mult), [xsbR[ti], r_dtb], [r_xdd])

                    def state(ti):
                        rows, col0 = tiles[ti]
                        mk_xdd(ti)
                        S.op("pe", lambda e: e.matmul(PS[7][:, 0:512], lhsT=xsb[0:rows, ti, 512:640], rhs=xddf[0:rows, :],
                                                      start=True, stop=True), [xsbR[ti], r_xdd], [PSR[7]])
                        if ti == 0:
                            S.op("dve", lambda e: e.tensor_copy(hf_, PS[7][:, 0:512]), [PSR[7]], [r_h])
                            if main:
                                S.op("dve", lambda e: e.tensor_scalar(out=t1f, in0=fp_all[:, 512 * g:512 * g + 512], scalar1=flags[:, 1:2],
                                                                      scalar2=None, op0=ALU.mult), [r_fp[g], r_flags], [r_t1])
                                S.op("dve", lambda e: e.scalar_tensor_tensor(out=hf_, in0=hf_, scalar=flags[:, 0:1], in1=t1f,
                                                                             op0=ALU.mult, op1=ALU.add), [r_h, r_t1, r_flags], [r_h])
                        else:
                            S.op("pool", lambda e: e.tensor_tensor(out=t1[:], in0=hst[:],
                                                                   in1=dtb[:, CDB, ti, hsl].unsqueeze(2).to_broadcast([128, 8, 64]),
                                                                   op=ALU.mult), [r_h, r_dtb], [r_t1])
                            S.op("dve", lambda e: e.tensor_tensor(out=hf_, in0=t1f, in1=PS[7][:, 0:512], op=ALU.add),
                                 [r_t1, PSR[7]], [r_h])
                        S.op("act", lambda e: e.copy(hbf[:], hf_), [r_h], [r_hb])
                        last_prompt = (ti == nt - 2) if main else (ti == nt - 1)
                        if last_prompt and not main:
                            S.op("act", lambda e: e.copy(fp_all[:, 512 * g:512 * g + 512], hf_), [r_h], [r_fp[g]])
                        if last_prompt and main:
                            def _trst(e):
                                ins = None
                                for c in range(4):
                                    ins = e.transpose(PS[4][:, c * 128:(c + 1) * 128], hf_[:, c * 128:(c + 1) * 128], ident_f)
                                return ins
                            S.op("pe", _trst, [r_h, r_cst], [PSR[4]])
                            S.op("act", lambda e: e.copy(sto[:].rearrange("p c n -> p (c n)"), PS[4][:, 0:512]), [PSR[4]], [r_sto])
                            S.dma("sp", o_ssm_p.rearrange("(c q) n -> q c n", q=128)[:, 4 * g:4 * g + 4, :], sto[:], reads=[r_sto])

                    def sample_setup(ti):
                        rows, col0 = tiles[ti]
                        S.dma("pool", ctm[:].rearrange("p j t -> p (j t)"), maskT_d.partition_broadcast(128), writes=[r_ctm])
                        S.op("dve", lambda e: e.tensor_tensor(out=ctm[:], in0=ctm[:],
                                                              in1=xbcT[:, 5, col0:col0 + 128].unsqueeze(1).to_broadcast([128, 16, 128]),
                                                              op=ALU.mult), [xbcTR[5], r_ctm], [r_ctm])
                        S.op("dve", lambda e: e.tensor_tensor(out=bmk,
                                                              in0=xsb[:, ti, 512:640].unsqueeze(1).to_broadcast([128, 16, 128]),
                                                              in1=seqm.unsqueeze(2).to_broadcast([128, 16, 128]), op=ALU.mult),
                             [xsbR[ti], r_cst], [r_bmk])
                        S.op("dve", lambda e: e.tensor_copy(cdx_s, dtb[:, CDB, ti, hsl].unsqueeze(2).to_broadcast([128, 8, 64])),
                             [r_dtb], [r_sx])
                        cdxf = cdx_s.rearrange("p h q -> p (h q)")

                        def _cdm(e):
                            ins = None
                            for c in range(4):
                                ins = e.matmul(PS[4][:, c * 16:(c + 1) * 16], lhsT=cdxf[:, c * 128:(c + 1) * 128], rhs=seqm,
                                               start=True, stop=True)
                            return ins
                        S.op("pe", _cdm, [r_sx, r_cst], [PSR[4]])
                        S.op("act", lambda e: e.activation(out=cdcol[:].rearrange("p c j -> p (c j)"), in_=PS[4][:, 0:64],
                                                           func=AF.Copy, scale=0.125), [PSR[4]], [r_cdc])
                        S.op("pool", lambda e: e.tensor_tensor(out=xdd_s, in0=xsb[:, ti, 0:512].rearrange("p (h q) -> p h q", q=64),
                                                               in1=dtb[:, WGT, ti, hsl].unsqueeze(2).to_broadcast([128, 8, 64]),
                                                               op=ALU.mult), [xsbR[ti], r_dtb], [r_sx])
                        for jq in range(2):
                            ld(jq)

                    def ld(jq):
                        b_ = jq % 6
                        S.dma("sp", h0n[b_], sssm_d[jq, 512 * g:512 * g + 512, :].rearrange("(c q) n -> q c n", q=128),
                              writes=[h0nR[b_]] + h0nX[b_])

                    def sample(ti):
                        rows, col0 = tiles[ti]
                        ymm(ti)
                        xddsf = xdd_s.rearrange("p h q -> p (h q)")
                        NHB = 6
                        for jq in range(2, 4):
                            ld(jq)
                        for jq in range(16):
                            hb_ = jq % NHB
                            tb_ = jq % 2
                            pbs = jq % 3

                            def _trh(e):
                                ins = None
                                for c in range(4):
                                    ins = e.transpose(PS[7][:, c * 128:(c + 1) * 128], h0n[hb_][:, c, :], ident_f)
                                return ins
                            S.op("pe", _trh, [h0nR[hb_], r_cst], [PSR[7]])
                            S.op("act", lambda e: e.copy(h0T[tb_][:], PS[7][:, 0:512]), [PSR[7]], [h0TR[tb_]])
                            S.op("pe", lambda e: e.matmul(PS[6][:, 0:512], lhsT=ctm[:, jq, :], rhs=h0T[tb_][:],
                                                          start=(jq == 0), stop=(jq == 15)), [r_ctm, h0TR[tb_]], [PSR[6]])

                            def _smm(e):
                                ins = None
                                for c in range(4):
                                    ins = e.matmul(PS[pbs][:, c * 128:(c + 1) * 128], lhsT=xddsf[:, c * 128:(c + 1) * 128],
                                                   rhs=bmk[:, jq, :], start=True, stop=True)
                                return ins
                            S.op("pe", _smm, [r_sx, r_bmk], [PSR[pbs]])
                            S.op("pool", lambda e: e.tensor_tensor(out=h0n[hb_], in0=h0n[hb_],
                                                                   in1=cdcol[:, :, jq].unsqueeze(2).to_broadcast([128, 4, 128]),
                                                                   op=ALU.mult), [h0nR[hb_], r_cdc], [h0nR[hb_]])
                            S.op("dve", lambda e: e.tensor_tensor(out=h0n[hb_], in0=h0n[hb_],
                                                                  in1=PS[pbs][:, 0:512].rearrange("p (c n) -> p c n", n=128),
                                                                  op=ALU.add), [h0nR[hb_], PSR[pbs]], [h0nR[hb_]])
                            S.dma("sp", o_ssm_s[jq, 512 * g:512 * g + 512, :].rearrange("(c q) n -> q c n", q=128), h0n[hb_],
                                  reads=[h0nR[hb_]])
                            if jq + 4 < 16:
                                ld(jq + 4)
                        post(ti)
                        post2(ti)

                    if not main:
                        t1b = t1[:].rearrange("p h q -> p (h q)").bitcast(BF16)
                        xdd4 = [xdd[:], t1b[:, 0:512].rearrange("p (h q) -> p h q", q=64), t1b[:, 512:1024].rearrange("p (h q) -> p h q", q=64),
                                hbf[:].rearrange("p (h q) -> p h q", q=64)]
                        xddR4 = [r_xdd, R(), R(), r_hb]
                        for ti in range(nt):
                            rows, col0 = tiles[ti]
                            xb_ = ti % 4
                            S.op("pool", lambda e: e.tensor_tensor(out=xdd4[xb_][0:rows], in0=xsb[0:rows, ti, 0:512].rearrange("p (h q) -> p h q", q=64),
                                                                   in1=dtb[0:rows, WGT, ti, hsl].unsqueeze(2).to_broadcast([rows, 8, 64]),
                                                                   op=ALU.mult), [xsbR[ti], r_dtb], [xddR4[xb_]])
                            S.op("pe", lambda e: e.matmul(PS[7][:, 0:512], lhsT=xsb[0:rows, ti, 512:640],
                                                          rhs=xdd4[xb_][0:rows].rearrange("p h q -> p (h q)"),
                                                          start=(ti == 0), stop=(ti == nt - 1)), [xsbR[ti], xddR4[xb_]], [PSR[7]])
                        S.op("act", lambda e: e.copy(fp_all[:, 512 * g:512 * g + 512], PS[7][:, 0:512]), [PSR[7]], [r_fp[g]])
                    else:
                        state(0)
                        mprep(1)
                        mprep2(1)
                        zproj(1)
                        sample_setup(nt - 1)
                        for ti in range(1, nt - 1):
                            zproj(ti + 1)
                            ymm(ti)
                            S.op("pe", lambda e: e.matmul(PS[6][:, 0:512], lhsT=xbcT[:, 5, tiles[ti][1]:tiles[ti][1] + 128], rhs=hbf[:],
                                                          start=True, stop=True), [xbcTR[5], r_hb], [PSR[6]])
                            state(ti)
                            mprep(ti + 1)
                            post(ti)
                            mprep2(ti + 1)
                            post2(ti)
                            if ti > 1:
                                post_b(ti - 1)
                        post_b(nt - 2)
                        sample(nt - 1)
                        post_b(nt - 1)
                S.barrier()

        st_pm = ExitStack()
        fp_all = sbt(st_pm, "fp_all", [128, D], F32)
        sub_p = ExitStack()
        with sub_p:
            xnT_p = sbt(sub_p, "xnT_p", [128, 16, TP], BF16)
            xnT_pR = [R() for _ in tiles_p]
            phase_a(sub_p, xnT_p, xnT_pR, tiles_p, TM, 10)
            ssd_segment("prefix", xnT_p, xnT_pR, tiles_p)
            dump("fp_all", fp_all[:], [128, D], r_fp)
        if stop_after == "P":
            S.finish()
            st_pm.close()
            return nc, dbg_out

        ssd_segment("main", xnT_m, xnT_mR, tiles_m)
        st_pm.close()
        dump("yT", yT, [128, 16, TF], [r for l in yTR for r in l], BF16)
        if stop_after == "M1":
            S.finish()
            return nc, dbg_out

        yTall = [r for l in yTR for r in l]
        st2 = ExitStack()
        poT = sbt(st2, "poT", [128, 16, TF], BF16)
        poTR = [R() for _ in range(16)]
        fblocks = [(0, 512), (512, 1024), (1024, 1152)]
        sub = ExitStack()
        with sub:
            um = sbt(sub, "um", [128, 1040], F32)
            r_um = R()
            us = sbt(sub, "us", [128, 16, 23], F32)
            r_us = R()
            pa = [sbt(sub, "pa%d" % i, [128, 1040], F32) for i in range(2)]
            paR = [R(), R()]
            pas = [sbt(sub, "pas%d" % i, [128, 16, 23], F32) for i in range(2)]
            pasR = [R(), R()]
            upre = sbt(sub, "upre", [128, 143], F32)
            r_upre = R()
            pst = sbt(sub, "pst", [128, 4, 128], F32)
            r_pst = R()
            ppt = sbt(sub, "ppt", [15, 4, 128], F32)
            r_ppt = R()
            spt = sbt(sub, "spt", [120, 2, 512], F32)
            r_spt = R()
            pooledT = sbt(sub, "pooledT", [128, 4, TF], BF16)
            pooledR = [R() for _ in range(4)]
            zpT = sbt(sub, "zpT", [128, 4, TF], BF16)
            zpR = [R() for _ in range(4)]
            tq = sbt(sub, "tq", [128, 512], F32)
            r_tq = R()
            S.dma("sp", o_pool_s.rearrange("(j r) c -> j r c", r=15)[:, 0:7, :],
                  spool_d.rearrange("(j r) c -> j r c", r=15)[:, 8:15, :])
            ublocks = [(0, 512), (512, 1024), (1024, TM)]
            psrot2 = [0]

            def nps():
                b = psrot2[0] % 3
                psrot2[0] += 1
                return b
            for gi in range(4):
                w = (2, 4, 8, 16)[gi]
                for hh in range(2):
                    S.dma("sp", spt[:, hh, :], spool_d[120 * hh:120 * hh + 120, 512 * gi:512 * gi + 512], writes=[r_spt])
                wcur = None
                for j in range(4):
                    c = 4 * gi + j
                    if j % 2 == 0:
                        wcur = w_next()
                    wt, wr = wcur
                    sj = j % 2
                    def _trh2(e):
                        ins = None
                        for hh in range(2):
                            ins = e.transpose(PS[6][:, hh * 120:(hh + 1) * 120], spt[0:120, hh, j * 128:(j + 1) * 128],
                                              ident_f[0:120, 0:120])
                        return ins
                    S.op("pe", _trh2, [r_spt, r_cst], [PSR[6]])
                    S.op("act", lambda e: e.copy(us[:, :, 0:15], PS[6][:, 0:240].rearrange("p (j r) -> p j r", r=15)),
                         [PSR[6]], [r_us])
                    for bi, (c0, c1) in enumerate(ublocks):
                        n = c1 - c0
                        pb = nps()

                        def _mm(e):
                            ins = None
                            for k in range(16):
                                ins = e.matmul(PS[pb][:, 0:n], lhsT=wt[:, k, sj * 128:(sj + 1) * 128], rhs=xnT_m[:, k, c0:c1],
                                               start=(k == 0), stop=(k == 15))
                            return ins
                        S.op("pe", _mm, list(xnT_mR) + list(wr), [PSR[pb]])
                        if bi < 2:
                            S.op("act", lambda e: e.copy(um[:, c0:c1], PS[pb][:, 0:n]), [PSR[pb]], [r_um])
                        else:
                            S.op("act", lambda e: e.copy(um[:, 1024:1040], PS[pb][:, 0:16]), [PSR[pb]], [r_um])
                            S.op("act", lambda e: e.copy(us[:, :, 15:23], PS[pb][:, 16:144].rearrange("p (j i) -> p j i", i=8)),
                                 [PSR[pb]], [r_us])
                            S.op("act", lambda e: e.copy(upre[:, 15:143].rearrange("p (i j) -> p i j", j=16),
                                                         PS[pb][:, 16:144].rearrange("p (j i) -> p i j", i=8)), [PSR[pb]], [r_upre])
                            S.op("act", lambda e: e.copy(upre[:, 0:15], PS[pb][:, 1:16]), [PSR[pb]], [r_upre])
                    def _tru(e):
                        e.transpose(PS[7][:, 0:128], upre[:, 15:143], ident_f)
                        return e.transpose(PS[7][0:15, 128:256], upre[:, 0:15], ident_f)
                    S.op("pe", _tru, [r_upre, r_cst], [PSR[7]])
                    S.op("act", lambda e: e.copy(pst[:, j, :], PS[7][:, 0:128]), [PSR[7]], [r_pst])
                    S.op("act", lambda e: e.copy(ppt[:, j, :], PS[7][0:15, 128:256]), [PSR[7]], [r_ppt])
                    src, srcR = um, r_um
                    srcs, srcsR = us, r_us
                    tot = 0
                    step = 1
                    bi_ = 0
                    while step < w:
                        lo = tot + step
                        dst, dstR = pa[bi_], paR[bi_]
                        dsts, dstsR = pas[bi_], pasR[bi_]
                        S.op("dve", lambda e: e.tensor_tensor(out=dst[:, lo:1040], in0=src[:, lo:1040], in1=src[:, lo - step:1040 - step],
                                                              op=ALU.add), [srcR], [dstR])
                        S.op("dve", lambda e: e.tensor_tensor(out=dsts[:, :, lo:23], in0=srcs[:, :, lo:23],
                                                              in1=srcs[:, :, lo - step:23 - step], op=ALU.add), [srcsR], [dstsR])
                        src, srcR, srcs, srcsR = dst, dstR, dsts, dstsR
                        tot = lo
                        step *= 2
                        bi_ ^= 1
                    S.op("dve", lambda e: e.scalar_tensor_tensor(out=pooledT[:, j, 0:1024], in0=src[:, 16:1040], scalar=1.0 / w,
                                                                 in1=um[:, 16:1040], op0=ALU.mult, op1=ALU.subtract),
                         [srcR, r_um], [pooledR[j]])
                    S.op("dve", lambda e: e.scalar_tensor_tensor(out=pooledT[:, j, 1024:1152].rearrange("p (j i) -> p j i", i=8),
                                                                 in0=srcs[:, :, 15:23], scalar=1.0 / w, in1=us[:, :, 15:23],
                                                                 op0=ALU.mult, op1=ALU.subtract), [srcsR, r_us], [pooledR[j]])
                ops_ = o_pool_s.rearrange("(j r) c -> j r c", r=15)
                for i8 in range(8):
                    S.dma("sp", ops_[:, 7 + i8, 512 * gi:512 * gi + 512].rearrange("j (c q) -> j c q", q=128),
                          pst[16 * i8:16 * i8 + 16, :, :], reads=[r_pst])
                S.dma("sp", o_pool_p[:, 512 * gi:512 * gi + 512].rearrange("r (c q) -> r c q", q=128), ppt[:], reads=[r_ppt])
                for j in range(4):
                    if j % 2 == 0:
                        wcur = w_next()
                    wt, wr = wcur
                    sj = j % 2
                    for (c0, c1) in fblocks:
                        n = c1 - c0
                        pb = nps()

                        def _mm(e):
                            ins = None
                            for k in range(16):
                                ins = e.matmul(PS[pb][:, 0:n], lhsT=wt[:, k, sj * 128:(sj + 1) * 128],
                                               rhs=xnT_m[:, k, 16 + c0:16 + c1], start=(k == 0), stop=(k == 15))
                            return ins
                        S.op("pe", _mm, list(xnT_mR) + list(wr), [PSR[pb]])
                        S.op("act", lambda e: e.activation(out=zpT[:, j, c0:c1], in_=PS[pb][:, 0:n], func=AF.Silu),
                             [PSR[pb]], [zpR[j]])
                for dj in range(4):
                    if dj % 2 == 0:
                        wcur = w_next()
                    wt, wr = wcur
                    sj = dj % 2
                    c = 4 * gi + dj
                    for (c0, c1) in fblocks:
                        n = c1 - c0
                        pb = nps()

                        def _mm(e):
                            ins = None
                            for k in range(4):
                                ins = e.matmul(PS[pb][:, 0:n], lhsT=wt[:, k, sj * 128:(sj + 1) * 128], rhs=pooledT[:, k, c0:c1],
                                               start=(k == 0), stop=(k == 3))
                            return ins
                        S.op("pe", _mm, pooledR + list(wr), [PSR[pb]])
                        S.op("act", lambda e: e.activation(out=tq[:, 0:n], in_=PS[pb][:, 0:n], func=AF.Identity,
                                                           scale=cp[:, 136 + c:137 + c], bias=cp[:, 120 + c:121 + c]),
                             [PSR[pb], r_cp], [r_tq])
                        S.op("dve", lambda e: e.tensor_tensor(out=poT[:, c, c0:c1], in0=tq[:, 0:n], in1=zpT[:, dj, c0:c1], op=ALU.mult),
                             [r_tq, zpR[dj]], [poTR[c]])
            S.barrier()
        dump("poT", poT[:], [128, 16, TF], poTR, BF16)
        if stop_after == "M2":
            S.finish()
            st2.close()
            return nc, dbg_out

        mergedT = sbt(st2, "mergedT", [128, 16, TF], BF16)
        mergedR = [R() for _ in range(16)]
        sub = ExitStack()
        with sub:
            m1 = sbt(sub, "m1", [128, 2, TF], F32)
            r_m1 = R()
            sg = [sbt(sub, "sg%d" % i, [128, 512], F32) for i in range(2)]
            sgR = [R(), R()]
            psrot3 = [0]

            def nps3():
                b = psrot3[0] % 6
                psrot3[0] += 1
                return b
            for mb in range(8):
                for term in range(2):
                    wa, wra = w_next()
                    wg, wrg = w_next(keep=1)
                    act_src, act_R = (yT, yTall) if term == 0 else (poT[:], poTR)
                    for sc in range(2):
                        mc = 2 * mb + sc
                        for (c0, c1) in fblocks:
                            n = c1 - c0
                            pa_, pg_ = nps3(), nps3()

                            def _mma(e):
                                ins = None
                                for k in range(16):
                                    ins = e.matmul(PS[pa_][:, 0:n], lhsT=wa[:, k, sc * 128:(sc + 1) * 128], rhs=act_src[:, k, c0:c1],
                                                   start=(k == 0), stop=(k == 15))
                                return ins
                            S.op("pe", _mma, list(act_R) + list(wra), [PSR[pa_]])

                            def _mmg(e):
                                ins = None
                                for k in range(16):
                                    ins = e.matmul(PS[pg_][:, 0:n], lhsT=wg[:, k, sc * 128:(sc + 1) * 128],
                                                   rhs=xnT_m[:, k, 16 + c0:16 + c1], start=(k == 0), stop=(k == 15))
                                return ins
                            S.op("pe", _mmg, list(xnT_mR) + list(wrg), [PSR[pg_]])
                            sb_ = (psrot3[0] // 2) % 2
                            S.op("act", lambda e: e.activation(out=sg[sb_][:, 0:n], in_=PS[pg_][:, 0:n], func=AF.Sigmoid),
                                 [PSR[pg_]], [sgR[sb_]])
                            if term == 0:
                                S.op("dve", lambda e: e.tensor_tensor(out=m1[:, sc, c0:c1], in0=sg[sb_][:, 0:n], in1=PS[pa_][:, 0:n],
                                                                      op=ALU.mult), [sgR[sb_], PSR[pa_]], [r_m1])
                            else:
                                S.op("dve", lambda e: e.tensor_tensor(out=sg[sb_][:, 0:n], in0=sg[sb_][:, 0:n], in1=PS[pa_][:, 0:n],
                                                                      op=ALU.mult), [sgR[sb_], PSR[pa_]], [sgR[sb_]])
                                S.op("dve", lambda e: e.tensor_tensor(out=mergedT[:, mc, c0:c1], in0=sg[sb_][:, 0:n], in1=m1[:, sc, c0:c1],
                                                                      op=ALU.add), [sgR[sb_], r_m1], [mergedR[mc]])
            S.barrier()
        dump("mergedT", mergedT[:], [128, 16, TF], mergedR, BF16)
        if stop_after == "F":
            S.finish()
            st2.close()
            return nc, dbg_out

        sub = ExitStack()
        with sub:
            hn = xy[:, 0:9 * 2 * D].bitcast(F32).rearrange("p (t d) -> p t d", d=D)
            hnR = [R() for _ in range(9)]
            fnw = sbt(sub, "fnw_bc", [128, D], F32)
            r_fnw = R()
            S.dma("sp", fnw[:], fnw_d.partition_broadcast(128), writes=[r_fnw])
            junk = sbt(sub, "junk_g", [128, D], BF16)
            r_junk = R()
            gs = sbt(sub, "gs", [128, 4, 16], F32)
            r_gs = R()
            S.op("dve", lambda e: e.memset(gs[:], 0.0), [], [r_gs])
            for tt in range(9):
                S.dma("sp", hn[:, tt, :], xin[16 + 128 * tt:16 + 128 * tt + 128, :], writes=[hnR[tt]])
            psrot4 = [0]
            for db in range(8):
                wt, wr = w_next()
                for tt in range(9):
                    pb = psrot4[0] % 6
                    psrot4[0] += 1

                    def _mm(e):
                        ins = None
                        for k in range(16):
                            ins = e.matmul(PS[pb][:, 0:256], lhsT=mergedT[:, k, tt * 128:(tt + 1) * 128], rhs=wt[:, k, 0:256],
                                           start=(k == 0), stop=(k == 15))
                        return ins
                    S.op("pe", _mm, mergedR + list(wr), [PSR[pb]])
                    S.op("dve", lambda e: e.tensor_tensor(out=hn[:, tt, db * 256:(db + 1) * 256], in0=hn[:, tt, db * 256:(db + 1) * 256],
                                                          in1=PS[pb][:, 0:256], op=ALU.add), [hnR[tt], PSR[pb]], [hnR[tt]])
            for tt in range(9):
                S.op("act", lambda e: e.activation(out=junk[:], in_=hn[:, tt, :], func=AF.Square, accum_out=gs[:, 0, tt:tt + 1]),
                     [hnR[tt]], [r_junk, r_gs])
            S.op("dve", lambda e: e.tensor_scalar(out=gs[:, 1, 0:9], in0=gs[:, 0, 0:9], scalar1=1.0 / D, scalar2=EPS,
                                                  op0=ALU.mult, op1=ALU.add), [r_gs], [r_gs])
            S.op("act", lambda e: e.activation(out=gs[:, 2, 0:9], in_=gs[:, 1, 0:9], func=AF.Sqrt), [r_gs], [r_gs])
            S.op("dve", lambda e: e.reciprocal(gs[:, 3, 0:9], gs[:, 2, 0:9]), [r_gs], [r_gs])
            for tt in range(9):
                S.op("dve", lambda e: e.scalar_tensor_tensor(out=hn[:, tt, :], in0=hn[:, tt, :], scalar=gs[:, 3, tt:tt + 1], in1=fnw[:],
                                                             op0=ALU.mult, op1=ALU.mult), [hnR[tt], r_gs, r_fnw], [hnR[tt]])
                S.dma("sp", y_d[128 * tt:128 * tt + 128, :], hn[:, tt, :], reads=[hnR[tt]])
            S.barrier()
        st2.close()

        S.finish()
    return nc, dbg_out


def prep_inputs(x_prompt, x_sample, state_conv, state_ssm, state_pool, meta_tokens, norm_w, w_in,
                conv_w, conv_b, dt_bias, a_log, d_skip, ssd_norm_w, w_proj_ssd, pool_mix_w,
                pool_mix_b, pool_scale, w_proj_pool, w_out, final_norm_w):
    f = np.float32
    cst, mT = _consts()
    shared = {
        "w_in": np.ascontiguousarray(w_in[0], f), "w_ps": np.ascontiguousarray(w_proj_ssd[0], f),
        "w_pp": np.ascontiguousarray(w_proj_pool[0], f), "w_out": np.ascontiguousarray(w_out[0], f),
        "pmw": np.ascontiguousarray(pool_mix_w[0], f), "norm_w": np.ascontiguousarray(norm_w[0], f),
        "fnw": np.ascontiguousarray(final_norm_w, f), "ssdnw": np.ascontiguousarray(ssd_norm_w[0], f),
        "hvec": np.ascontiguousarray(np.stack([dt_bias[0], a_log[0], d_skip[0]]), f),
        "colp1": np.ascontiguousarray(np.concatenate([conv_w[0].reshape(4 * 24, 128), conv_b[0].reshape(24, 128)]), f),
        "colp2": np.ascontiguousarray(np.concatenate([pool_mix_b[0].reshape(16, 128), pool_scale[0].reshape(16, 128)]), f),
        "cst": cst, "maskT": np.ascontiguousarray(mT.reshape(-1)),
    }
    maps = []
    zeros_p = np.zeros((TP, D), f)
    for i in range(NCORES):
        b, hf = i // 2, i % 2
        if hf == 0:
            xin = np.concatenate([meta_tokens, x_prompt[b, 0:1024], x_sample[16 * i:16 * i + 16].reshape(128, D), zeros_p])
            fl = np.tile(np.array([[1.0, 0.0]], f), (128, 1))
        else:
            xin = np.concatenate([x_prompt[b, 1008:1024], x_prompt[b, 1024:2048], x_sample[16 * i:16 * i + 16].reshape(128, D),
                                  meta_tokens, x_prompt[b, 0:1024]])
            fl = np.tile(np.array([[0.0, 1.0]], f), (128, 1))
        m = dict(shared)
        m["xin"] = np.ascontiguousarray(xin, f)
        m["flags"] = fl
        m["sconv"] = np.ascontiguousarray(state_conv[0, 16 * i:16 * i + 16].reshape(48, 3072), f)
        m["spool"] = np.ascontiguousarray(state_pool[0, 16 * i:16 * i + 16].reshape(240, D), f)
        m["sssm"] = np.ascontiguousarray(state_ssm[0, 16 * i:16 * i + 16].reshape(16, D, 128), f)
        maps.append(m)
    return maps


_NC_CACHE = {}


def kernel(**inputs):
    inputs = {k: np.asarray(v) for k, v in inputs.items()}
    maps = prep_inputs(**inputs)
    if "nc" not in _NC_CACHE:
        _NC_CACHE["nc"] = build_program()[0]
    nc = _NC_CACHE["nc"]
    res = run_bass_kernel_spmd(nc, maps, core_ids=list(range(NCORES)))
    f = np.float32
    y_prompt = np.zeros((4, 2048, D), f)
    y_sample = np.zeros((128, 8, D), f)
    ncp = np.zeros((1, 4, 3, 3072), f)
    nsp = np.zeros((1, 4, NH, 64, 128), f)
    npp = np.zeros((1, 4, 15, D), f)
    ncs = np.zeros((1, 128, 3, 3072), f)
    nss = np.zeros((1, 128, NH, 64, 128), f)
    nps_ = np.zeros((1, 128, 15, D), f)
    for i, r in enumerate(res.results):
        b, hf = i // 2, i % 2
        y = np.asarray(r["y"])
        y_prompt[b, 1024 * hf:1024 * hf + 1024] = y[:1024]
        y_sample[16 * i:16 * i + 16] = y[1024:].reshape(16, 8, D)
        if hf == 1:
            ncp[0, b] = np.asarray(r["o_conv_p"])
            nsp[0, b] = np.asarray(r["o_ssm_p"]).reshape(NH, 64, 128)
            npp[0, b] = np.asarray(r["o_pool_p"])
        ncs[0, 16 * i:16 * i + 16] = np.asarray(r["o_conv_s"]).reshape(16, 3, 3072)
        nss[0, 16 * i:16 * i + 16] = np.asarray(r["o_ssm_s"]).reshape(16, NH, 64, 128)
        nps_[0, 16 * i:16 * i + 16] = np.asarray(r["o_pool_s"]).reshape(16, 15, D)
    return (y_prompt, y_sample, ncp, nsp, npp, ncs, nss, nps_)
```

```python
import numpy as np
from contextlib import ExitStack
import concourse.bass as bass
import concourse.mybir as mybir
from concourse.bass_utils import run_bass_kernel_spmd

F32 = mybir.dt.float32
BF16 = mybir.dt.bfloat16
AF = mybir.ActivationFunctionType
ALU = mybir.AluOpType

D = 2048
NH = 32
DIN = 13344
OFF_Z = 0
OFF_XBC = 2048
OFF_B = OFF_XBC + 2048
OFF_C = OFF_B + 512
OFF_DT = 5120
OFF_ZP = 5152
OFF_U = 7200
OFF_G1 = 9248
OFF_G2 = 11296
EPS = 1e-6
NCORES = 8
TM = 1168
TP = 1040
TF = 1152
WCOLS = 256
NWB = 3


class R:
    __slots__ = ("w", "r", "name", "excl")

    def __init__(self, name="", excl=False):
        self.w = None
        self.r = {}
        self.name = name
        self.excl = excl


class Sched:
    def __init__(self, nc, es):
        self.nc = nc
        self.eng = {"pe": nc.tensor, "act": nc.scalar, "dve": nc.vector, "pool": nc.gpsimd, "sp": nc.sync}
        self.semh = {}
        self.cnt = {}
        self.seen = {k: {} for k in self.eng}
        for k in self.eng:
            self.semh[k] = es.enter_context(nc.semaphore("s_" + k))
            self.cnt[k] = 0
        self.dslots = {}
        self.dnext = {}
        for q, n in (("sp", 12), ("pool", 8), ("act", 4)):
            self.dslots[q] = []
            for i in range(n):
                key = "d_%s%d" % (q, i)
                self.semh[key] = es.enter_context(nc.semaphore(key))
                self.cnt[key] = 0
                self.dslots[q].append(key)
            self.dnext[q] = 0
        self.nwaits = 0
        self.nops = 0

    def _deps(self, reads, writes, eng=None):
        deps = {}

        def add(k, v):
            if deps.get(k, 0) < v:
                deps[k] = v
        for r in reads:
            if r.w is not None:
                add(*r.w)
            if r.excl:
                for k, v in r.r.items():
                    if k != eng:
                        add(k, v)
        for w in writes:
            if w.w is not None:
                add(*w.w)
            for k, v in w.r.items():
                add(k, v)
        return deps

    def _wait(self, eng, deps):
        seen = self.seen[eng]
        for k, v in deps.items():
            if k == eng and eng in ("pe", "sp"):
                continue
            if seen.get(k, 0) < v:
                self.eng[eng].wait_ge(self.semh[k], v)
                seen[k] = v
                self.nwaits += 1

    def _mark(self, me, reads, writes):
        for w in writes:
            w.w = me
            w.r = {}
        ws = set(id(w) for w in writes)
        for r in reads:
            if id(r) not in ws:
                if r.r.get(me[0], 0) < me[1]:
                    r.r[me[0]] = me[1]

    def op(self, eng, fn, reads=(), writes=()):
        self._wait(eng, self._deps(reads, writes, eng))
        ins = fn(self.eng[eng])
        self.cnt[eng] += 1
        ins.then_inc(self.semh[eng], 1)
        self._mark((eng, self.cnt[eng]), reads, writes)
        self.nops += 1

    def dma(self, q, out, in_, reads=(), writes=()):
        deps = self._deps(reads, writes)
        slots = self.dslots[q]
        key = slots[self.dnext[q] % len(slots)]
        self.dnext[q] += 1
        if self.cnt[key] > 0:
            deps[key] = max(deps.get(key, 0), self.cnt[key])
        self._wait(q, deps)
        ins = self.eng[q].dma_start(out=out, in_=in_)
        self.cnt[key] += 16
        ins.then_inc(self.semh[key], 16)
        self._mark((key, self.cnt[key]), reads, writes)

    def barrier(self):
        for e in self.eng:
            deps = {k: v for k, v in self.cnt.items() if v > 0}
            self._wait(e, deps)

    def finish(self):
        self.barrier()


def _consts():
    c = np.zeros((128, 6 * 128 + 16), np.float32)
    i = np.arange(128)
    c[:, 0:128] = np.eye(128)
    c[:, 128:256] = (i[:, None] <= i[None, :])
    same = (i[:, None] // 8) == (i[None, :] // 8)
    c[:, 256:384] = (i[:, None] <= i[None, :]) & same
    c[:, 384:512] = same
    c[:, 512:640] = 1.0
    c[:, 640:704] = ((i[:, None] % 64) == np.arange(64)[None, :])
    c[:, 768:784] = (i[:, None] // 8) == np.arange(16)[None, :]
    c[:, 704:768] = (i[:, None] < 16)
    mT = ((np.arange(128)[None, :] // 8) == np.arange(16)[:, None]).astype(np.float32)
    return c, mT


def build_program(stop_after=None, dbg=()):
    nc = bass.Bass("TRN2", target_bir_lowering=False)
    dt_ = nc.dram_tensor

    def din(name, shape):
        return dt_(name, list(shape), F32, kind="ExternalInput").ap()

    def dout(name, shape):
        return dt_(name, list(shape), F32, kind="ExternalOutput").ap()

    xin = din("xin", [TM + TP, D])
    flags_d = din("flags", [128, 2])
    w_in = din("w_in", [D, DIN])
    w_ps = din("w_ps", [D, D])
    w_pp = din("w_pp", [D, D])
    w_out = din("w_out", [D, D])
    pmw = din("pmw", [4, 512, 512])
    normw_d = din("norm_w", [D])
    fnw_d = din("fnw", [D])
    ssdnw_d = din("ssdnw", [D])
    hvec_d = din("hvec", [3, NH])
    colp1_d = din("colp1", [120, 128])
    colp2_d = din("colp2", [32, 128])
    sconv_d = din("sconv", [48, 3072])
    spool_d = din("spool", [240, D])
    sssm_d = din("sssm", [16, D, 128])
    cst_d = din("cst", [128, 784])
    maskT_d = din("maskT", [16 * 128])

    y_d = dout("y", [TF, D])
    o_conv_p = dout("o_conv_p", [3, 3072])
    o_ssm_p = dout("o_ssm_p", [D, 128])
    o_pool_p = dout("o_pool_p", [15, D])
    o_conv_s = dout("o_conv_s", [48, 3072])
    o_ssm_s = dout("o_ssm_s", [16, D, 128])
    o_pool_s = dout("o_pool_s", [240, D])
    acscr = dt_("acscr", [10, NH * 128], F32, kind="Internal").ap()
    dbg_out = {}

    es = ExitStack()
    with es:
        S = Sched(nc, es)

        uid = [0]

        def sbt(stack, name, shape, dt):
            uid[0] += 1
            return stack.enter_context(nc.sbuf_tensor("sb%d_%s" % (uid[0], name), list(shape), dt))

        def dump(name, ap, shape, reads, dt=F32):
            if name in dbg:
                o = dt_("dbg_" + name, list(shape), dt, kind="ExternalOutput").ap()
                dbg_out[name] = o
                S.dma("sp", o, ap, reads=reads)

        PS = [es.enter_context(nc.psum_tensor("ps%d" % i, [128, 512], F32)) for i in range(8)]
        PSR = [R("ps%d" % i, excl=True) for i in range(8)]

        cst = sbt(es, "cst", [128, 784], F32)
        r_cst = R()
        S.dma("sp", cst[:], cst_d, writes=[r_cst])
        ident_f = cst[:, 0:128]
        tri = cst[:, 128:256]
        tri_b = cst[:, 256:384]
        ones_b = cst[:, 384:512]
        ones = cst[:, 512:640]
        i64 = cst[:, 640:704]
        seqm = cst[:, 768:784]
        ones16 = cst[:, 704:768]
        ident_bf = sbt(es, "ident_bf", [128, 128], BF16)
        r_idb = R()
        S.op("dve", lambda e: e.tensor_copy(ident_bf[:], ident_f), [r_cst], [r_idb])
        flags = sbt(es, "flags_s", [128, 2], F32)
        r_flags = R()
        S.dma("sp", flags[:], flags_d, writes=[r_flags])
        hv = sbt(es, "hv", [128, 4, NH], F32)
        r_hv = R()
        for j in range(3):
            S.dma("sp", hv[:, j, :], hvec_d[j].partition_broadcast(128), writes=[r_hv])
        S.op("act", lambda e: e.activation(out=hv[:, 3, :], in_=hv[:, 1, :], func=AF.Exp), [r_hv], [r_hv])
        S.op("dve", lambda e: e.tensor_scalar(out=hv[:, 3, :], in0=hv[:, 3, :], scalar1=-1.0, scalar2=None,
                                              op0=ALU.mult), [r_hv], [r_hv])
        dmat = sbt(es, "dmat", [128, NH, 64], BF16)
        r_dmat = R()
        S.op("dve", lambda e: e.tensor_tensor(out=dmat[:], in0=hv[:, 2, :].unsqueeze(2).to_broadcast([128, NH, 64]),
                                              in1=i64.unsqueeze(1).to_broadcast([128, NH, 64]), op=ALU.mult),
             [r_hv, r_cst], [r_dmat])
        cp = sbt(es, "cp", [128, 160], F32)
        r_cp = R()
        wb = [sbt(es, "wb%d" % i, [128, 16, WCOLS], BF16) for i in range(NWB)]
        stat = sbt(es, "stat", [128, 4, 24], F32)
        xy = sbt(es, "xy", [128, 16 * TM + 16 * TF], BF16)
        xnT_m = xy[:, 0:16 * TM].rearrange("p (k t) -> p k t", t=TM)
        yT = xy[:, 16 * TM:16 * TM + 16 * TF].rearrange("p (k t) -> p k t", t=TF)
        cp_stack = ExitStack()
        cp_tok = sbt(cp_stack, "cp_tok", [128, 2, 128], F32)
        r_cpt = R()
        S.dma("sp", cp_tok[0:120, 0, :], colp1_d, writes=[r_cpt])
        S.dma("sp", cp_tok[0:32, 1, :], colp2_d, writes=[r_cpt])

        def _tr_cp(e):
            e.transpose(PS[7][:, 0:120], cp_tok[0:120, 0, :], ident_f[0:120, 0:120])
            return e.transpose(PS[7][:, 128:160], cp_tok[0:32, 1, :], ident_f[0:32, 0:32])
        S.op("pe", _tr_cp, [r_cpt, r_cst], [PSR[7]])
        S.op("act", lambda e: e.copy(cp[:, 0:120], PS[7][:, 0:120]), [PSR[7]], [r_cp])
        S.op("act", lambda e: e.copy(cp[:, 120:152], PS[7][:, 128:160]), [PSR[7]], [r_cp])
        S.op("dve", lambda e: e.tensor_tensor(out=cp[:, 120:136], in0=cp[:, 120:136], in1=cp[:, 136:152], op=ALU.mult),
             [r_cp], [r_cp])
        S.barrier()
        cp_stack.close()
        r_fp = [R() for _ in range(4)]

        wbR = [[R() for _ in range(4)] for _ in range(NWB)]
        wsched = []
        wstate = {"issued": 0, "used": 0}

        def w_issue(upto):
            while wstate["issued"] < min(upto, len(wsched)):
                i = wstate["issued"]
                mat, c0, ncol, nk = wsched[i]
                slot = i % NWB
                src = mat.rearrange("(k p) e -> p k e", p=128)
                nq = 4 if nk == 16 else 1
                kq = nk // nq
                for q in range(nq):
                    S.dma("pool", wb[slot][:, q * kq:(q + 1) * kq, 0:ncol], src[:, q * kq:(q + 1) * kq, c0:c0 + ncol],
                          writes=[wbR[slot][q]])
                wstate["issued"] += 1

        def w_next(keep=0):
            i = wstate["used"]
            w_issue(i + NWB - keep)
            wstate["used"] += 1
            slot = i % NWB
            nk = wsched[i][3]
            return wb[slot], (wbR[slot] if nk == 16 else wbR[slot][0:1])

        def w_add(mat, c0, ncol, nk=16):
            wsched.append((mat, c0, ncol, nk))

        w_add(w_in, OFF_DT, 32)
        for g in range(4):
            w_add(w_in, OFF_XBC + 512 * g, 256)
            w_add(w_in, OFF_XBC + 512 * g + 256, 256)
            w_add(w_in, OFF_B + 128 * g, 128)
        w_add(w_in, OFF_DT, 32)
        for g in range(4):
            w_add(w_in, OFF_XBC + 512 * g, 256)
            w_add(w_in, OFF_XBC + 512 * g + 256, 256)
            w_add(w_in, OFF_B + 128 * g, 128)
            w_add(w_in, OFF_C + 128 * g, 128)
            w_add(w_in, OFF_Z + 512 * g, 256)
            w_add(w_in, OFF_Z + 512 * g + 256, 256)
        for g in range(4):
            w_add(w_in, OFF_U + 512 * g, 256)
            w_add(w_in, OFF_U + 512 * g + 256, 256)
            w_add(w_in, OFF_ZP + 512 * g, 256)
            w_add(w_in, OFF_ZP + 512 * g + 256, 256)
            w_add(pmw[g], 0, 256, 4)
            w_add(pmw[g], 256, 256, 4)
        for mb in range(8):
            w_add(w_ps, 256 * mb, 256)
            w_add(w_in, OFF_G1 + 256 * mb, 256)
            w_add(w_pp, 256 * mb, 256)
            w_add(w_in, OFF_G2 + 256 * mb, 256)
        for db in range(8):
            w_add(w_out, 256 * db, 256)

        r_stat = R()
        S.op("dve", lambda e: e.memset(stat[:], 0.0), [], [r_stat])

        def phase_a(stack, xnT, xnT_R, tiles, row_base, slot0):
            sub = ExitStack()
            with sub:
                vecbc = sbt(sub, "normw_bc", [128, D], F32)
                r_vec = R()
                S.dma("sp", vecbc[:], normw_d.partition_broadcast(128), writes=[r_vec])
                NXB = 3
                xt = [sbt(sub, "xt%d" % i, [128, D], F32) for i in range(NXB)]
                xtR = [R() for _ in range(NXB)]
                xnb = [sbt(sub, "xnb%d" % i, [128, D], BF16) for i in range(NXB)]
                xnbR = [R() for _ in range(NXB)]
                junk = sbt(sub, "junk_a", [128, D], BF16)
                r_junk = R()
                nt = len(tiles)
                stR = [R() for _ in range(nt)]
                for r_ in stR:
                    r_.w = r_stat.w

                def stats1(ti):
                    rows, col0 = tiles[ti]
                    b = ti % NXB
                    sl = slot0 + ti
                    S.dma("sp", xt[b][0:rows, :], xin[row_base + col0:row_base + col0 + rows, :], writes=[xtR[b]])
                    S.op("act", lambda e: e.activation(out=junk[0:rows, :], in_=xt[b][0:rows, :], func=AF.Square,
                                                       accum_out=stat[0:rows, 0, sl:sl + 1]),
                         [xtR[b]], [r_junk, stR[ti]])

                def stats2(ti):
                    rows, col0 = tiles[ti]
                    sl = slot0 + ti
                    S.op("dve", lambda e: e.tensor_scalar(out=stat[0:rows, 1, sl:sl + 1], in0=stat[0:rows, 0, sl:sl + 1],
                                                          scalar1=1.0 / D, scalar2=EPS, op0=ALU.mult, op1=ALU.add),
                         [stR[ti]], [stR[ti]])
                    S.op("act", lambda e: e.activation(out=stat[0:rows, 2, sl:sl + 1], in_=stat[0:rows, 1, sl:sl + 1],
                                                       func=AF.Sqrt), [stR[ti]], [stR[ti]])
                    S.op("dve", lambda e: e.reciprocal(stat[0:rows, 3, sl:sl + 1], stat[0:rows, 2, sl:sl + 1]),
                         [stR[ti]], [stR[ti]])
                stats1(0)
                stats2(0)
                for ti, (rows, col0) in enumerate(tiles):
                    b = ti % NXB
                    sl = slot0 + ti
                    if ti + 1 < nt:
                        stats1(ti + 1)
                    S.op("dve", lambda e: e.scalar_tensor_tensor(out=xnb[b][0:rows, :], in0=xt[b][0:rows, :],
                                                                 scalar=stat[0:rows, 3, sl:sl + 1], in1=vecbc[0:rows, :],
                                                                 op0=ALU.mult, op1=ALU.mult),
                         [xtR[b], stR[ti], r_vec], [xnbR[b]])
                    pbs_ = []
                    for half in range(2):
                        pb = (0, 1, 2, 3, 5, 6)[(2 * ti + half) % 6]
                        pbs_.append(pb)
                        psb = PS[pb][:].bitcast(BF16)

                        def _tr(e, half=half, psb=psb):
                            ins = None
                            for kk in range(8):
                                k = half * 8 + kk
                                ins = e.transpose(psb[:, kk * 128:kk * 128 + rows], xnb[b][0:rows, k * 128:(k + 1) * 128],
                                                  ident_bf[0:rows, 0:rows])
                            return ins
                        S.op("pe", _tr, [xnbR[b], r_idb], [PSR[pb]])
                    if ti + 1 < nt:
                        stats2(ti + 1)
                    for half in range(2):
                        pb = pbs_[half]
                        psb = PS[pb][:].bitcast(BF16)
                        src = psb.rearrange("p (k t) -> p k t", t=128)[:, :, 0:rows]
                        dst = xnT[:, half * 8:(half + 1) * 8, col0:col0 + rows]
                        if half == 0:
                            S.op("act", lambda e: e.copy(dst, src), [PSR[pb]], [xnT_R[ti]])
                        else:
                            S.op("dve", lambda e: e.tensor_copy(dst, src), [PSR[pb]], [xnT_R[ti]])
                S.barrier()

        tiles_p = [(16, 0)] + [(128, 16 + 128 * i) for i in range(8)]
        tiles_m = [(16, 0)] + [(128, 16 + 128 * i) for i in range(8)] + [(128, 1040)]

        xnT_mR = [R() for _ in tiles_m]
        phase_a(es, xnT_m, xnT_mR, tiles_m, 0, 0)
        dump("xnT_m", xnT_m, [128, 16, TM], xnT_mR, BF16)

        if stop_after == "A":
            S.finish()
            return nc, dbg_out

        yTR = [[R() for _ in range(9)] for _ in range(4)]

        def ssd_segment(kind, xnT, xnTR, tiles):
            main = kind == "main"
            nt = len(tiles)
            ncols = TM if main else TP
            tblocks = [(0, 512), (512, 1024), (1024, ncols)]
            sub = ExitStack()
            with sub:
                acc_pre = sbt(sub, "acc", [128, 1040], F32)
                r_acc_pre = R()
                dtb = sbt(sub, "dtb", [128, 6, nt, NH], F32)
                r_dtb = R()
                DTA, DT, BNEG, AC, CDB, WGT = range(6)
                X, LNDT, TOT, EA = DTA, BNEG, CDB, DTA
                wt, wr = w_next()

                def _dtmm(e):
                    ins = None
                    for ti, (rows, col0) in enumerate(tiles):
                        for k in range(16):
                            ins = e.matmul(PS[4][:, ti * NH:(ti + 1) * NH], lhsT=xnT[:, k, col0:col0 + 128],
                                           rhs=wt[:, k, 0:NH], start=(k == 0), stop=(k == 15))
                    return ins
                S.op("pe", _dtmm, list(xnTR) + list(wr), [PSR[4]])
                psdt = PS[4][:, 0:nt * NH].rearrange("p (t h) -> p t h", h=NH)

                def bc_h(j):
                    return hv[:, j, :].unsqueeze(1).to_broadcast([128, nt, NH])
                S.op("dve", lambda e: e.tensor_tensor(out=dtb[:, X], in0=psdt, in1=bc_h(0), op=ALU.add),
                     [PSR[4], r_hv], [r_dtb])
                S.op("dve", lambda e: e.tensor_scalar(out=dtb[:, WGT], in0=dtb[:, X], scalar1=0.0, scalar2=None, op0=ALU.max),
                     [r_dtb], [r_dtb])
                S.op("dve", lambda e: e.scalar_tensor_tensor(out=dtb[:, DT], in0=dtb[:, WGT], scalar=-2.0, in1=dtb[:, X],
                                                             op0=ALU.mult, op1=ALU.add), [r_dtb], [r_dtb])
                S.op("act", lambda e: e.activation(out=dtb[:, DT], in_=dtb[:, DT], func=AF.Exp), [r_dtb], [r_dtb])
                S.op("act", lambda e: e.activation(out=dtb[:, DT], in_=dtb[:, DT], func=AF.Ln, bias=1.0),
                     [r_dtb], [r_dtb])
                S.op("dve", lambda e: e.tensor_tensor(out=dtb[:, DT], in0=dtb[:, DT], in1=dtb[:, WGT], op=ALU.add),
                     [r_dtb], [r_dtb])
                S.op("dve", lambda e: e.tensor_tensor(out=dtb[:, DTA], in0=dtb[:, DT], in1=bc_h(3), op=ALU.mult),
                     [r_dtb, r_hv], [r_dtb])
                S.op("dve", lambda e: e.tensor_scalar(out=dtb[:, DTA, 0, :], in0=dtb[:, DTA, 0, :], scalar1=ones16[:, 0:1],
                                                      scalar2=None, op0=ALU.mult), [r_dtb, r_cst], [r_dtb])
                S.op("act", lambda e: e.activation(out=dtb[:, LNDT], in_=dtb[:, DT], func=AF.Ln, bias=1e-30), [r_dtb], [r_dtb])

                r_acs = [R() for _ in range(nt)]

                def pre2():
                    def _cum(e):
                        ins = None
                        for ti, (rows, col0) in enumerate(tiles):
                            smp = main and ti == nt - 1
                            tr = tri_b if smp else tri
                            on = ones_b if smp else ones
                            e.matmul(PS[4][:, ti * NH:(ti + 1) * NH], lhsT=tr, rhs=dtb[:, DTA, ti, :], start=True, stop=True)
                            ins = e.matmul(PS[7][:, ti * NH:(ti + 1) * NH], lhsT=on, rhs=dtb[:, DTA, ti, :], start=True, stop=True)
                        return ins
                    S.op("pe", _cum, [r_dtb, r_cst], [PSR[4], PSR[7]])
                    S.op("act", lambda e: e.copy(dtb[:, AC], PS[4][:, 0:nt * NH].rearrange("p (t h) -> p t h", h=NH)),
                         [PSR[4]], [r_dtb])
                    S.op("act", lambda e: e.copy(dtb[:, TOT], PS[7][:, 0:nt * NH].rearrange("p (t h) -> p t h", h=NH)),
                         [PSR[7]], [r_dtb])
                    S.op("dve", lambda e: e.tensor_tensor(out=dtb[:, BNEG], in0=dtb[:, LNDT], in1=dtb[:, AC], op=ALU.subtract),
                         [r_dtb], [r_dtb])
                    S.op("dve", lambda e: e.tensor_tensor(out=dtb[:, WGT], in0=dtb[:, TOT], in1=dtb[:, AC], op=ALU.subtract),
                         [r_dtb], [r_dtb])
                    S.op("act", lambda e: e.activation(out=dtb[:, WGT], in_=dtb[:, WGT], func=AF.Exp), [r_dtb], [r_dtb])
                    S.op("dve", lambda e: e.tensor_tensor(out=dtb[:, WGT], in0=dtb[:, WGT], in1=dtb[:, DT], op=ALU.mult),
                         [r_dtb], [r_dtb])
                    S.op("act", lambda e: e.activation(out=dtb[:, CDB], in_=dtb[:, TOT], func=AF.Exp), [r_dtb], [r_dtb])
                    if not main:
                        for ti in range(nt - 2, -1, -1):
                            src_w = dtb[:, CDB, ti + 1, :]
                            if ti < nt - 2:
                                S.op("dve", lambda e: e.tensor_tensor(out=dtb[:, BNEG, ti, :], in0=dtb[:, BNEG, ti + 1, :], in1=src_w,
                                                                      op=ALU.mult), [r_dtb], [r_dtb])
                            else:
                                S.op("dve", lambda e: e.tensor_copy(dtb[:, BNEG, ti, :], src_w), [r_dtb], [r_dtb])
                            S.op("dve", lambda e: e.tensor_tensor(out=dtb[:, WGT, ti, :], in0=dtb[:, WGT, ti, :], in1=dtb[:, BNEG, ti, :],
                                                                  op=ALU.mult), [r_dtb], [r_dtb])
                    if main:
                        acT = acc_pre[0:NH, 0:384].rearrange("p (i t) -> p i t", t=128)
                        r_acT = r_acc_pre
                        for t0 in range(1, nt, 3):
                            tl = list(range(t0, min(t0 + 3, nt)))
                            pb = 5

                            def _acT(e):
                                ins = None
                                for ii, ti in enumerate(tl):
                                    smp = ti == nt - 1
                                    ins = e.matmul(PS[pb][0:NH, ii * 128:(ii + 1) * 128], lhsT=dtb[:, DTA, ti, :],
                                                   rhs=(tri_b if smp else tri), start=True, stop=True)
                                return ins
                            S.op("pe", _acT, [r_dtb, r_cst], [PSR[pb]])
                            S.op("act", lambda e: e.copy(acT[:, 0:len(tl), :],
                                                         PS[pb][0:NH, 0:len(tl) * 128].rearrange("p (i t) -> p i t", t=128)),
                                 [PSR[pb]], [r_acT])
                            for ii, ti in enumerate(tl):
                                S.dma("sp", acscr[ti].rearrange("(h t) -> h t", t=128), acT[:, ii, :], reads=[r_acT],
                                      writes=[r_acs[ti]])
                    if main:
                        S.op("act", lambda e: e.activation(out=dtb[:, EA], in_=dtb[:, AC], func=AF.Exp), [r_dtb], [r_dtb])
                    dump("dtb_" + kind, dtb[:], [128, 6, nt, NH], [r_dtb])

                extm = [sbt(sub, "extm", [128, 3 + 1040], F32)] * 2
                extmR = [R()] * 2
                acc = acc_pre
                r_acc = r_acc_pre
                for i in range(2):
                    S.op("dve", lambda e: e.memset(extm[i][:, 0:3], 0.0), [], [extmR[i]])
                nch = 6 if main else 5
                xbcT = sbt(sub, "xbcT", [128, nch, ncols], BF16)
                xbcTR = [R() for _ in range(nch)]
                xsb = sbt(sub, "xsb", [128, nt, 640], BF16)
                xsbR = [R() for _ in range(nt)]
                hst = sbt(sub, "hst", [128, 8, 64], F32)
                r_h = R()
                hbf = sbt(sub, "hbf", [128, 512], BF16)
                r_hb = R()
                t1 = sbt(sub, "t1", [128, 8, 64], F32)
                r_t1 = R()
                xdd = sbt(sub, "xdd", [128, 8, 64], BF16)
                r_xdd = R()
                if main:
                    exts = [sbt(sub, "exts", [128, 16, 11], F32)] * 2
                    extsR = [R()] * 2
                    accs = sbt(sub, "accs", [128, 16, 8], F32)
                    r_accs = R()
                    pre_s = sbt(sub, "pre_s", [128, 131], F32)
                    r_pre = R()
                    pcs_t = sbt(sub, "pcs", [128, 6, 128], F32)
                    pcs = pcs_t[0:48]
                    pcp = pcs_t[64:67]
                    r_pcs = R()
                    r_pcp = R()
                    zs = sbt(sub, "zs", [128, 9, 512], BF16)
                    zsR = [R() for _ in range(9)]
                    acbc = [sbt(sub, "acbc%d" % i, [128, 8, 128], F32) for i in range(2)]
                    acbcR = [R(), R()]
                    Mq = sbt(sub, "Mq", [128, 8, 128], BF16)
                    cbm = sbt(sub, "cbm", [128, 128], F32)
                    Mq2 = [Mq[:], pcs_t[:].rearrange("p a b -> p (a b)")[:, 0:512].bitcast(BF16).rearrange("p (h t) -> p h t", t=128)]
                    MqR = [R(), r_pcs]
                    cbm2 = [cbm[:], pre_s[:, 0:128]]
                    cbmR = [R(), r_pre]
                    yv = sbt(sub, "yv", [128, 8, 64], F32)
                    r_yv = R()
                    yn = sbt(sub, "yn", [128, 512], BF16)
                    junk = sbt(sub, "junk_m", [128, 512], BF16)
                    r_junk = R()
                    zth = extm[0][:, 272:528].bitcast(BF16)
                    r_zth = R()
                    cdx_s = extm[0][:, 528:1040].rearrange("p (h q) -> p h q", q=64)
                    xdd_s = extm[0][:, 528:784].bitcast(BF16).rearrange("p (h q) -> p h q", q=64)
                    r_sx = R()
                    negh = sbt(sub, "negh", [128, 1], F32)
                    r_negh = R()
                    S.op("dve", lambda e: e.memset(negh[:], -0.5), [], [r_negh])
                    yn2 = [yn[:], extm[0][:, 16:272].bitcast(BF16)]
                    ynR = [R(), extmR[0]]
                    gst = sbt(sub, "gst", [128, 4, 40], F32)
                    r_gst = R()
                    S.op("dve", lambda e: e.memset(gst[:], 0.0), [], [r_gst])
                    nwbc = sbt(sub, "nwbc", [128, 512], F32)
                    r_nw = R()
                    sto = yv
                    r_sto = r_yv
                    scT = sbt(sub, "scT", [128, 6, 48], F32)
                    r_scT = R()
                    ctm = sbt(sub, "ctm", [128, 16, 128], BF16)
                    r_ctm = R()
                    bmk = acc[:, 0:1024].bitcast(BF16).rearrange("p (j n) -> p j n", n=128)
                    r_bmk = r_acc
                    h0n_t = [sbt(sub, "h0n%d" % i, [128, 4, 128], F32) for i in range(2)]
                    h0n = [h0n_t[0][:], h0n_t[1][:], acbc[0][:, 0:4, :], acbc[0][:, 4:8, :], acbc[1][:, 0:4, :], acbc[1][:, 4:8, :]]
                    h0nR = [R() for _ in range(6)]
                    h0nX = [[], [], [acbcR[0]], [acbcR[0]], [acbcR[1]], [acbcR[1]]]
                    h0T = [sbt(sub, "h0T%d" % i, [128, 512], BF16) for i in range(2)]
                    h0TR = [R(), R()]
                    cdx = t1
                    r_cdx = r_t1
                    cdcol = sbt(sub, "cdcol", [128, 4, 16], F32)
                    r_cdc = R()

                psrot = [0]

                def next_ps():
                    b = psrot[0] % 3
                    psrot[0] += 1
                    return b

                for g in range(4):
                    if main:
                        S.dma("sp", nwbc[:], ssdnw_d[512 * g:512 * g + 512].partition_broadcast(128), writes=[r_nw])
                        for jj, (cc0, ncc) in enumerate(((512 * g, 512), (2048 + 128 * g, 128), (2560 + 128 * g, 128))):
                            o0 = 0 if jj == 0 else (512 if jj == 1 else 640)
                            S.dma("sp", acc[0:48, o0:o0 + ncc], sconv_d[:, cc0:cc0 + ncc], writes=[r_acc])

                        def _trs(e):
                            ins = None
                            for ii in range(6):
                                ins = e.transpose(PS[6][:, ii * 48:(ii + 1) * 48], acc[0:48, ii * 128:(ii + 1) * 128],
                                                  ident_f[0:48, 0:48])
                            return ins
                        S.op("pe", _trs, [r_acc, r_cst], [PSR[6]])
                        S.op("act", lambda e: e.copy(scT[:], PS[6][:, 0:288].rearrange("p (i t) -> p i t", t=48)),
                             [PSR[6]], [r_scT])
                    chunk_list = [(0, 0), (0, 1), (1, 0), (1, 1), (2, 0)] + ([(3, 0)] if main else [])
                    wcur = {}
                    for j, (wi, sub_j) in enumerate(chunk_list):
                        if wi not in wcur:
                            wcur.clear()
                            wcur[wi] = w_next()
                        wt, wr = wcur[wi]
                        if j < 4:
                            cglob = 4 * g + j
                        elif j == 4:
                            cglob = 16 + g
                        else:
                            cglob = 20 + g
                        eb = j % 2
                        for bi, (c0, c1) in enumerate(tblocks):
                            n = c1 - c0
                            pb = next_ps()

                            def _mm(e):
                                ins = None
                                for k in range(16):
                                    ins = e.matmul(PS[pb][:, 0:n], lhsT=wt[:, k, sub_j * 128:(sub_j + 1) * 128],
                                                   rhs=xnT[:, k, c0:c1], start=(k == 0), stop=(k == 15))
                                return ins
                            S.op("pe", _mm, list(xnTR) + list(wr), [PSR[pb]])
                            if bi < 2:
                                S.op("act", lambda e: e.copy(extm[eb][:, 3 + c0:3 + c1], PS[pb][:, 0:n]),
                                     [PSR[pb]], [extmR[eb]] + ([r_zth, r_sx] if main else []))
                            else:
                                S.op("act", lambda e: e.copy(extm[eb][:, 3 + 1024:3 + 1040], PS[pb][:, 0:16]),
                                     [PSR[pb]], [extmR[eb]])
                                if main:
                                    S.op("act", lambda e: e.copy(pre_s[:, 0:128].rearrange("p (i j) -> p i j", j=16),
                                                                 PS[pb][:, 16:144].rearrange("p (j i) -> p i j", i=8)), [PSR[pb]], [r_pre])
                                    S.op("act", lambda e: e.copy(pre_s[:, 128:131], PS[pb][:, 13:16]), [PSR[pb]], [r_pre])
                                    S.op("dve", lambda e: e.tensor_copy(exts[eb][:, :, 3:11],
                                                                         PS[pb][:, 16:144].rearrange("p (j i) -> p j i", i=8)),
                                         [PSR[pb]], [extsR[eb]])
                                    S.op("dve", lambda e: e.tensor_copy(exts[eb][:, :, 0:3],
                                                                         scT[:, j, :].rearrange("p (j r) -> p j r", r=3)),
                                         [r_scT], [extsR[eb]])
                        L = 1040

                        def cw(k):
                            return cp[:, k * 24 + cglob:k * 24 + cglob + 1]
                        S.op("dve", lambda e: e.tensor_scalar(out=acc[:, 0:L], in0=extm[eb][:, 0:L], scalar1=cw(0), scalar2=None,
                                                              op0=ALU.mult), [extmR[eb], r_cp], [r_acc])
                        for k in range(1, 4):
                            S.op("dve", lambda e: e.scalar_tensor_tensor(out=acc[:, 0:L], in0=extm[eb][:, k:k + L], scalar=cw(k),
                                                                         in1=acc[:, 0:L], op0=ALU.mult, op1=ALU.add),
                                 [extmR[eb], r_cp, r_acc], [r_acc])
                        S.op("act", lambda e: e.activation(out=xbcT[:, j, 0:L], in_=acc[:, 0:L], func=AF.Silu,
                                                           bias=cp[:, 96 + cglob:97 + cglob]), [r_acc, r_cp], [xbcTR[j]])
                        if main:
                            S.op("dve", lambda e: e.tensor_scalar(out=accs[:], in0=exts[eb][:, :, 0:8], scalar1=cw(0), scalar2=None,
                                                                  op0=ALU.mult), [extsR[eb], r_cp], [r_accs])
                            for k in range(1, 4):
                                S.op("dve", lambda e: e.scalar_tensor_tensor(out=accs[:], in0=exts[eb][:, :, k:k + 8], scalar=cw(k),
                                                                             in1=accs[:], op0=ALU.mult, op1=ALU.add),
                                     [extsR[eb], r_cp, r_accs], [r_accs])
                            S.op("act", lambda e: e.activation(out=xbcT[:, j, L:L + 128].rearrange("p (j i) -> p j i", i=8),
                                                               in_=accs[:], func=AF.Silu, bias=cp[:, 96 + cglob:97 + cglob]),
                                 [r_accs, r_cp], [xbcTR[j]])
                            S.op("pe", lambda e: e.transpose(PS[7][0:51, 0:128], pre_s[:, 80:131], ident_f), [r_pre, r_cst], [PSR[7]])
                            S.op("act", lambda e: e.copy(pcs_t[0:51, j, :], PS[7][0:51, 0:128]), [PSR[7]], [r_pcs])
                    if main:
                        ocs = o_conv_s.rearrange("(j r) c -> j r c", r=3)
                        for (cc0, ncc, j0, j1) in ((512 * g, 512, 0, 4), (2048 + 128 * g, 128, 4, 5), (2560 + 128 * g, 128, 5, 6)):
                            for r3 in range(3):
                                S.dma("sp", ocs[:, r3, cc0:cc0 + ncc].rearrange("j (c q) -> j c q", q=128),
                                      pcs_t[16 * r3:16 * r3 + 16, j0:j1, :], reads=[r_pcs])
                            S.dma("sp", o_conv_p[:, cc0:cc0 + ncc].rearrange("r (c q) -> r c q", q=128), pcs_t[48:51, j0:j1, :],
                                  reads=[r_pcs])
                    if g == 0:
                        pre2()
                        dump("xbcT_" + kind, xbcT[:], [128, nch, ncols], xbcTR, BF16)

                    psb3 = PS[3][:].bitcast(BF16)
                    for ti, (rows, col0) in enumerate(tiles):
                        pbx = (3, 7)[ti % 2]
                        psbx = PS[pbx][:].bitcast(BF16)

                        def _trx(e):
                            ins = None
                            for jj in range(5):
                                ins = e.transpose(psbx[0:rows, jj * 128:(jj + 1) * 128], xbcT[:, jj, col0:col0 + rows], ident_bf[:])
                            return ins
                        S.op("pe", _trx, xbcTR[0:5] + [r_idb], [PSR[pbx]])
                        if ti % 2 == 0:
                            S.op("act", lambda e: e.copy(xsb[0:rows, ti, :], psbx[0:rows, 0:640]), [PSR[pbx]], [xsbR[ti]])
                        else:
                            S.op("dve", lambda e: e.tensor_copy(xsb[0:rows, ti, :], psbx[0:rows, 0:640]), [PSR[pbx]], [xsbR[ti]])

                    if main:
                        wz = [w_next(), w_next(keep=1)]

                        def zproj(ti):
                            rows, col0 = tiles[ti]
                            for half in range(2):
                                wt, wr = wz[half]
                                pb = next_ps()

                                def _zmm(e):
                                    ins = None
                                    for k in range(16):
                                        ins = e.matmul(PS[pb][:, 0:256], lhsT=xnT[:, k, col0:col0 + 128], rhs=wt[:, k, 0:256],
                                                       start=(k == 0), stop=(k == 15))
                                    return ins
                                S.op("pe", _zmm, [xnTR[ti]] + list(wr), [PSR[pb]])
                                S.op("act", lambda e: e.activation(out=zth[:, half * 256:(half + 1) * 256], in_=PS[pb][:, 0:256],
                                                                   func=AF.Tanh, scale=0.5), [PSR[pb]], [r_zth])
                                S.op("dve", lambda e: e.scalar_tensor_tensor(out=zs[:, ti - 1, half * 256:(half + 1) * 256],
                                                                             in0=zth[:, half * 256:(half + 1) * 256], scalar=1.0,
                                                                             in1=PS[pb][:, 0:256], op0=ALU.add, op1=ALU.mult),
                                     [r_zth, PSR[pb]], [zsR[ti - 1]])

                    hsl = slice(8 * g, 8 * g + 8)
                    xddf = xdd[:].rearrange("p h q -> p (h q)")
                    hf_ = hst[:].rearrange("p h q -> p (h q)")
                    t1f = t1[:].rearrange("p h q -> p (h q)")

                    def mprep(ti):
                        rows, col0 = tiles[ti]
                        smp = ti == nt - 1
                        bb = ti % 2
                        S.dma("sp", acbc[bb][:].rearrange("p h t -> p (h t)"),
                              acscr[ti, g * 1024:(g + 1) * 1024].partition_broadcast(128), reads=[r_acs[ti]],
                              writes=[acbcR[bb], h0nR[2 + 2 * bb], h0nR[3 + 2 * bb]])
                        S.op("dve", lambda e: e.tensor_tensor(out=acbc[bb][:], in0=acbc[bb][:],
                                                              in1=dtb[:, AC, ti, hsl].unsqueeze(2).to_broadcast([128, 8, 128]),
                                                              op=ALU.min), [acbcR[bb], r_dtb], [acbcR[bb]])
                        S.op("pool", lambda e: e.tensor_tensor(out=acbc[bb][:], in0=acbc[bb][:],
                                                               in1=dtb[:, BNEG, ti, hsl].unsqueeze(2).to_broadcast([128, 8, 128]),
                                                               op=ALU.add), [acbcR[bb], r_dtb], [acbcR[bb]])
                        S.op("act", lambda e: e.activation(out=acbc[bb][:], in_=acbc[bb][:], func=AF.Exp), [acbcR[bb]], [acbcR[bb]])
                        S.op("pe", lambda e: e.matmul(PS[4][:, 0:128], lhsT=xbcT[:, 4, col0:col0 + 128],
                                                      rhs=xbcT[:, 5, col0:col0 + 128], start=True, stop=True),
                             [xbcTR[4], xbcTR[5]], [PSR[4]])

                    def mprep2(ti):
                        rows, col0 = tiles[ti]
                        smp = ti == nt - 1
                        bb = ti % 2
                        S.op("dve", lambda e: e.tensor_tensor(out=cbm2[bb], in0=PS[4][:, 0:128], in1=(tri_b if smp else tri),
                                                              op=ALU.mult), [PSR[4], r_cst], [cbmR[bb]])
                        S.op("dve", lambda e: e.tensor_tensor(out=Mq2[bb], in0=acbc[bb][:],
                                                              in1=cbm2[bb].unsqueeze(1).to_broadcast([128, 8, 128]),
                                                              op=ALU.mult), [acbcR[bb], cbmR[bb]], [MqR[bb]])

                    def ymm(ti):
                        rows, col0 = tiles[ti]
                        bb = ti % 2

                        def _ymm(e):
                            ins = None
                            for h in range(8):
                                qb = 64 * (h % 2)
                                e.matmul(PS[5][:, h * 64:(h + 1) * 64], lhsT=Mq2[bb][:, h, :], rhs=xsb[:, ti, h * 64:(h + 1) * 64],
                                         start=True, stop=False)
                                ins = e.matmul(PS[5][:, h * 64:(h + 1) * 64], lhsT=xbcT[qb:qb + 64, h // 2, col0:col0 + 128],
                                               rhs=dmat[qb:qb + 64, 8 * g + h, :], start=False, stop=True)
                            return ins
                        S.op("pe", _ymm, [MqR[bb], xsbR[ti], r_dmat] + xbcTR[0:4], [PSR[5]])

                    def post(ti):
                        S.op("dve", lambda e: e.tensor_tensor(out=yv[:], in0=PS[6][:, 0:512].rearrange("p (h q) -> p h q", q=64),
                                                              in1=dtb[:, EA, ti, hsl].unsqueeze(2).to_broadcast([128, 8, 64]),
                                                              op=ALU.mult), [PSR[6], r_dtb], [r_yv])
                        yvf = yv[:].rearrange("p h q -> p (h q)")
                        S.op("dve", lambda e: e.tensor_tensor(out=yvf, in0=yvf, in1=PS[5][:, 0:512], op=ALU.add),
                             [r_yv, PSR[5]], [r_yv])
                        S.op("dve", lambda e: e.tensor_tensor(out=yvf, in0=yvf, in1=zs[:, ti - 1, :], op=ALU.mult),
                             [r_yv, zsR[ti - 1]], [r_yv])
                        sl = g * 10 + ti
                        S.op("act", lambda e: e.activation(out=junk[:], in_=yvf, func=AF.Square,
                                                           accum_out=gst[:, 0, sl:sl + 1]), [r_yv], [r_junk, r_gst])

                    def post2(ti):
                        yvf = yv[:].rearrange("p h q -> p (h q)")
                        sl = g * 10 + ti
                        S.op("dve", lambda e: e.tensor_scalar(out=gst[:, 1, sl:sl + 1], in0=gst[:, 0, sl:sl + 1],
                                                              scalar1=1.0 / 512, scalar2=4.0 * EPS, op0=ALU.mult, op1=ALU.add),
                             [r_gst], [r_gst])
                        S.op("pool", lambda e: e.tensor_tensor(out=gst[:, 3, sl:sl + 1], in0=gst[:, 1, sl:sl + 1], in1=negh[:],
                                                               op=ALU.pow), [r_gst, r_negh], [r_gst])
                        yb = ti % 2
                        S.op("dve", lambda e: e.scalar_tensor_tensor(out=yn2[yb], in0=yvf, scalar=gst[:, 3, sl:sl + 1],
                                                                     in1=nwbc[:], op0=ALU.mult,
                                                                     op1=ALU.mult), [r_yv, r_gst, r_nw], [ynR[yb]])

                    def post_b(ti):
                        yb = ti % 2

                        def _try(e):
                            ins = None
                            for c in range(4):
                                ins = e.transpose(psb3[:, c * 128:(c + 1) * 128], yn2[yb][:, c * 128:(c + 1) * 128], ident_bf[:])
                            return ins
                        S.op("pe", _try, [ynR[yb], r_idb], [PSR[3]])
                        tc0 = (ti - 1) * 128
                        S.op("act", lambda e: e.copy(yT[:, 4 * g:4 * g + 4, tc0:tc0 + 128],
                                                     psb3[:, 0:512].rearrange("p (c t) -> p c t", t=128)), [PSR[3]],
                             [yTR[g][ti - 1]])

                    def mk_xdd(ti):
                        rows, col0 = tiles[ti]
                        S.op("pool", lambda e: e.tensor_tensor(out=xdd[0:rows], in0=xsb[0:rows, ti, 0:512].rearrange("p (h q) -> p h q", q=64),
                                                               in1=dtb[0:rows, WGT, ti, hsl].unsqueeze(2).to_broadcast([rows, 8, 64]),
                                                               op=ALU.mult), [xsbR[ti], r_dtb], [r_xdd])

                    def state(ti):
                        rows, col0 = tiles[ti]
                        mk_xdd(ti)
                        S.op("pe", lambda e: e.matmul(PS[7][:, 0:512], lhsT=xsb[0:rows, ti, 512:640], rhs=xddf[0:rows, :],
                                                      start=True, stop=True), [xsbR[ti], r_xdd], [PSR[7]])
                        if ti == 0:
                            S.op("dve", lambda e: e.tensor_copy(hf_, PS[7][:, 0:512]), [PSR[7]], [r_h])
                            if main:
                                S.op("dve", lambda e: e.tensor_scalar(out=t1f, in0=fp_all[:, 512 * g:512 * g + 512], scalar1=flags[:, 1:2],
                                                                      scalar2=None, op0=ALU.mult), [r_fp[g], r_flags], [r_t1])
                                S.op("dve", lambda e: e.scalar_tensor_tensor(out=hf_, in0=hf_, scalar=flags[:, 0:1], in1=t1f,
                                                                             op0=ALU.mult, op1=ALU.add), [r_h, r_t1, r_flags], [r_h])
                        else:
                            S.op("pool", lambda e: e.tensor_tensor(out=t1[:], in0=hst[:],
                                                                   in1=dtb[:, CDB, ti, hsl].unsqueeze(2).to_broadcast([128, 8, 64]),
                                                                   op=ALU.mult), [r_h, r_dtb], [r_t1])
                            S.op("dve", lambda e: e.tensor_tensor(out=hf_, in0=t1f, in1=PS[7][:, 0:512], op=ALU.add),
                                 [r_t1, PSR[7]], [r_h])
                        S.op("act", lambda e: e.copy(hbf[:], hf_), [r_h], [r_hb])
                        last_prompt = (ti == nt - 2) if main else (ti == nt - 1)
                        if last_prompt and not main:
                            S.op("act", lambda e: e.copy(fp_all[:, 512 * g:512 * g + 512], hf_), [r_h], [r_fp[g]])
                        if last_prompt and main:
                            def _trst(e):
                                ins = None
                                for c in range(4):
                                    ins = e.transpose(PS[4][:, c * 128:(c + 1) * 128], hf_[:, c * 128:(c + 1) * 128], ident_f)
                                return ins
                            S.op("pe", _trst, [r_h, r_cst], [PSR[4]])
                            S.op("act", lambda e: e.copy(sto[:].rearrange("p c n -> p (c n)"), PS[4][:, 0:512]), [PSR[4]], [r_sto])
                            S.dma("sp", o_ssm_p.rearrange("(c q) n -> q c n", q=128)[:, 4 * g:4 * g + 4, :], sto[:], reads=[r_sto])

                    def sample_setup(ti):
                        rows, col0 = tiles[ti]
                        S.dma("pool", ctm[:].rearrange("p j t -> p (j t)"), maskT_d.partition_broadcast(128), writes=[r_ctm])
                        S.op("dve", lambda e: e.tensor_tensor(out=ctm[:], in0=ctm[:],
                                                              in1=xbcT[:, 5, col0:col0 + 128].unsqueeze(1).to_broadcast([128, 16, 128]),
                                                              op=ALU.mult), [xbcTR[5], r_ctm], [r_ctm])
                        S.op("dve", lambda e: e.tensor_tensor(out=bmk,
                                                              in0=xsb[:, ti, 512:640].unsqueeze(1).to_broadcast([128, 16, 128]),
                                                              in1=seqm.unsqueeze(2).to_broadcast([128, 16, 128]), op=ALU.mult),
                             [xsbR[ti], r_cst], [r_bmk])
                        S.op("dve", lambda e: e.tensor_copy(cdx_s, dtb[:, CDB, ti, hsl].unsqueeze(2).to_broadcast([128, 8, 64])),
                             [r_dtb], [r_sx])
                        cdxf = cdx_s.rearrange("p h q -> p (h q)")

                        def _cdm(e):
                            ins = None
                            for c in range(4):
                                ins = e.matmul(PS[4][:, c * 16:(c + 1) * 16], lhsT=cdxf[:, c * 128:(c + 1) * 128], rhs=seqm,
                                               start=True, stop=True)
                            return ins
                        S.op("pe", _cdm, [r_sx, r_cst], [PSR[4]])
                        S.op("act", lambda e: e.activation(out=cdcol[:].rearrange("p c j -> p (c j)"), in_=PS[4][:, 0:64],
                                                           func=AF.Copy, scale=0.125), [PSR[4]], [r_cdc])
                        S.op("pool", lambda e: e.tensor_tensor(out=xdd_s, in0=xsb[:, ti, 0:512].rearrange("p (h q) -> p h q", q=64),
                                                               in1=dtb[:, WGT, ti, hsl].unsqueeze(2).to_broadcast([128, 8, 64]),
                                                               op=ALU.mult), [xsbR[ti], r_dtb], [r_sx])
                        for jq in range(2):
                            ld(jq)

                    def ld(jq):
                        b_ = jq % 6
                        S.dma("sp", h0n[b_], sssm_d[jq, 512 * g:512 * g + 512, :].rearrange("(c q) n -> q c n", q=128),
                              writes=[h0nR[b_]] + h0nX[b_])

                    def sample(ti):
                        rows, col0 = tiles[ti]
                        ymm(ti)
                        xddsf = xdd_s.rearrange("p h q -> p (h q)")
                        NHB = 6
                        for jq in range(2, 4):
                            ld(jq)
                        for jq in range(16):
                            hb_ = jq % NHB
                            tb_ = jq % 2
                            pbs = jq % 3

                            def _trh(e):
                                ins = None
                                for c in range(4):
                                    ins = e.transpose(PS[7][:, c * 128:(c + 1) * 128], h0n[hb_][:, c, :], ident_f)
                                return ins
                            S.op("pe", _trh, [h0nR[hb_], r_cst], [PSR[7]])
                            S.op("act", lambda e: e.copy(h0T[tb_][:], PS[7][:, 0:512]), [PSR[7]], [h0TR[tb_]])
                            S.op("pe", lambda e: e.matmul(PS[6][:, 0:512], lhsT=ctm[:, jq, :], rhs=h0T[tb_][:],
                                                          start=(jq == 0), stop=(jq == 15)), [r_ctm, h0TR[tb_]], [PSR[6]])

                            def _smm(e):
                                ins = None
                                for c in range(4):
                                    ins = e.matmul(PS[pbs][:, c * 128:(c + 1) * 128], lhsT=xddsf[:, c * 128:(c + 1) * 128],
                                                   rhs=bmk[:, jq, :], start=True, stop=True)
                                return ins
                            S.op("pe", _smm, [r_sx, r_bmk], [PSR[pbs]])
                            S.op("pool", lambda e: e.tensor_tensor(out=h0n[hb_], in0=h0n[hb_],
                                                                   in1=cdcol[:, :, jq].unsqueeze(2).to_broadcast([128, 4, 128]),
                                                                   op=ALU.mult), [h0nR[hb_], r_cdc], [h0nR[hb_]])
                            S.op("dve", lambda e: e.tensor_tensor(out=h0n[hb_], in0=h0n[hb_],
                                                                  in1=PS[pbs][:, 0:512].rearrange("p (c n) -> p c n", n=128),
                                                                  op=ALU.add), [h0nR[hb_], PSR[pbs]], [h0nR[hb_]])
                            S.dma("sp", o_ssm_s[jq, 512 * g:512 * g + 512, :].rearrange("(c q) n -> q c n", q=128), h0n[hb_],
                                  reads=[h0nR[hb_]])
                            if jq + 4 < 16:
                                ld(jq + 4)
                        post(ti)
                        post2(ti)

                    if not main:
                        t1b = t1[:].rearrange("p h q -> p (h q)").bitcast(BF16)
                        xdd4 = [xdd[:], t1b[:, 0:512].rearrange("p (h q) -> p h q", q=64), t1b[:, 512:1024].rearrange("p (h q) -> p h q", q=64),
                                hbf[:].rearrange("p (h q) -> p h q", q=64)]
                        xddR4 = [r_xdd, R(), R(), r_hb]
                        for ti in range(nt):
                            rows, col0 = tiles[ti]
                            xb_ = ti % 4
                            S.op("pool", lambda e: e.tensor_tensor(out=xdd4[xb_][0:rows], in0=xsb[0:rows, ti, 0:512].rearrange("p (h q) -> p h q", q=64),
                                                                   in1=dtb[0:rows, WGT, ti, hsl].unsqueeze(2).to_broadcast([rows, 8, 64]),
                                                                   op=ALU.mult), [xsbR[ti], r_dtb], [xddR4[xb_]])
                            S.op("pe", lambda e: e.matmul(PS[7][:, 0:512], lhsT=xsb[0:rows, ti, 512:640],
                                                          rhs=xdd4[xb_][0:rows].rearrange("p h q -> p (h q)"),
                                                          start=(ti == 0), stop=(ti == nt - 1)), [xsbR[ti], xddR4[xb_]], [PSR[7]])
                        S.op("act", lambda e: e.copy(fp_all[:, 512 * g:512 * g + 512], PS[7][:, 0:512]), [PSR[7]], [r_fp[g]])
                    else:
                        state(0)
                        mprep(1)
                        mprep2(1)
                        zproj(1)
                        sample_setup(nt - 1)
                        for ti in range(1, nt - 1):
                            zproj(ti + 1)
                            ymm(ti)
                            S.op("pe", lambda e: e.matmul(PS[6][:, 0:512], lhsT=xbcT[:, 5, tiles[ti][1]:tiles[ti][1] + 128], rhs=hbf[:],
                                                          start=True, stop=True), [xbcTR[5], r_hb], [PSR[6]])
                            state(ti)
                            mprep(ti + 1)
                            post(ti)
                            mprep2(ti + 1)
                            post2(ti)
                            if ti > 1:
                                post_b(ti - 1)
                        post_b(nt - 2)
                        sample(nt - 1)
                        post_b(nt - 1)
                S.barrier()

        st_pm = ExitStack()
        fp_all = sbt(st_pm, "fp_all", [128, D], F32)
        sub_p = ExitStack()
        with sub_p:
            xnT_p = sbt(sub_p, "xnT_p", [128, 16, TP], BF16)
            xnT_pR = [R() for _ in tiles_p]
            phase_a(sub_p, xnT_p, xnT_pR, tiles_p, TM, 10)
            ssd_segment("prefix", xnT_p, xnT_pR, tiles_p)
            dump("fp_all", fp_all[:], [128, D], r_fp)
        if stop_after == "P":
            S.finish()
            st_pm.close()
            return nc, dbg_out

        ssd_segment("main", xnT_m, xnT_mR, tiles_m)
        st_pm.close()
        dump("yT", yT, [128, 16, TF], [r for l in yTR for r in l], BF16)
        if stop_after == "M1":
            S.finish()
            return nc, dbg_out

        yTall = [r for l in yTR for r in l]
        st2 = ExitStack()
        poT = sbt(st2, "poT", [128, 16, TF], BF16)
        poTR = [R() for _ in range(16)]
        fblocks = [(0, 384), (384, 768), (768, 1152)]
        sub = ExitStack()
        with sub:
            um = sbt(sub, "um", [128, 1040], F32)
            r_um = R()
            us = sbt(sub, "us", [128, 16, 23], F32)
            r_us = R()
            pa = [sbt(sub, "pa%d" % i, [128, 1040], F32) for i in range(2)]
            paR = [R(), R()]
            pas = [sbt(sub, "pas%d" % i, [128, 16, 23], F32) for i in range(2)]
            pasR = [R(), R()]
            upre = sbt(sub, "upre", [128, 143], F32)
            r_upre = R()
            pst = sbt(sub, "pst", [128, 4, 128], F32)
            r_pst = R()
            ppt = sbt(sub, "ppt", [15, 4, 128], F32)
            r_ppt = R()
            spt = sbt(sub, "spt", [120, 2, 512], F32)
            r_spt = R()
            pooledT = sbt(sub, "pooledT", [128, 4, TF], BF16)
            pooledR = [R() for _ in range(4)]
            zpT = sbt(sub, "zpT", [128, 4, TF], BF16)
            zpR = [R() for _ in range(4)]
            tq = sbt(sub, "tq", [128, 512], F32)
            r_tq = R()
            S.dma("sp", o_pool_s.rearrange("(j r) c -> j r c", r=15)[:, 0:7, :],
                  spool_d.rearrange("(j r) c -> j r c", r=15)[:, 8:15, :])
            ublocks = [(0, 512), (512, 1024), (1024, TM)]
            psrot2 = [0]

            def nps():
                b = psrot2[0] % 3
                psrot2[0] += 1
                return b
            for gi in range(4):
                w = (2, 4, 8, 16)[gi]
                for hh in range(2):
                    S.dma("sp", spt[:, hh, :], spool_d[120 * hh:120 * hh + 120, 512 * gi:512 * gi + 512], writes=[r_spt])
                wcur = None
                for j in range(4):
                    c = 4 * gi + j
                    if j % 2 == 0:
                        wcur = w_next()
                    wt, wr = wcur
                    sj = j % 2
                    def _trh2(e):
                        ins = None
                        for hh in range(2):
                            ins = e.transpose(PS[6][:, hh * 120:(hh + 1) * 120], spt[0:120, hh, j * 128:(j + 1) * 128],
                                              ident_f[0:120, 0:120])
                        return ins
                    S.op("pe", _trh2, [r_spt, r_cst], [PSR[6]])
                    S.op("act", lambda e: e.copy(us[:, :, 0:15], PS[6][:, 0:240].rearrange("p (j r) -> p j r", r=15)),
                         [PSR[6]], [r_us])
                    for bi, (c0, c1) in enumerate(ublocks):
                        n = c1 - c0
                        pb = nps()

                        def _mm(e):
                            ins = None
                            for k in range(16):
                                ins = e.matmul(PS[pb][:, 0:n], lhsT=wt[:, k, sj * 128:(sj + 1) * 128], rhs=xnT_m[:, k, c0:c1],
                                               start=(k == 0), stop=(k == 15))
                            return ins
                        S.op("pe", _mm, list(xnT_mR) + list(wr), [PSR[pb]])
                        if bi < 2:
                            S.op("act", lambda e: e.copy(um[:, c0:c1], PS[pb][:, 0:n]), [PSR[pb]], [r_um])
                        else:
                            S.op("act", lambda e: e.copy(um[:, 1024:1040], PS[pb][:, 0:16]), [PSR[pb]], [r_um])
                            S.op("act", lambda e: e.copy(us[:, :, 15:23], PS[pb][:, 16:144].rearrange("p (j i) -> p j i", i=8)),
                                 [PSR[pb]], [r_us])
                            S.op("act", lambda e: e.copy(upre[:, 15:143].rearrange("p (i j) -> p i j", j=16),
                                                         PS[pb][:, 16:144].rearrange("p (j i) -> p i j", i=8)), [PSR[pb]], [r_upre])
                            S.op("act", lambda e: e.copy(upre[:, 0:15], PS[pb][:, 1:16]), [PSR[pb]], [r_upre])
                    def _tru(e):
                        e.transpose(PS[7][:, 0:128], upre[:, 15:143], ident_f)
                        return e.transpose(PS[7][0:15, 128:256], upre[:, 0:15], ident_f)
                    S.op("pe", _tru, [r_upre, r_cst], [PSR[7]])
                    S.op("act", lambda e: e.copy(pst[:, j, :], PS[7][:, 0:128]), [PSR[7]], [r_pst])
                    S.op("act", lambda e: e.copy(ppt[:, j, :], PS[7][0:15, 128:256]), [PSR[7]], [r_ppt])
                    src, srcR = um, r_um
                    srcs, srcsR = us, r_us
                    tot = 0
                    step = 1
                    bi_ = 0
                    while step < w:
                        lo = tot + step
                        dst, dstR = pa[bi_], paR[bi_]
                        dsts, dstsR = pas[bi_], pasR[bi_]
                        S.op("dve", lambda e: e.tensor_tensor(out=dst[:, lo:1040], in0=src[:, lo:1040], in1=src[:, lo - step:1040 - step],
                                                              op=ALU.add), [srcR], [dstR])
                        S.op("dve", lambda e: e.tensor_tensor(out=dsts[:, :, lo:23], in0=srcs[:, :, lo:23],
                                                              in1=srcs[:, :, lo - step:23 - step], op=ALU.add), [srcsR], [dstsR])
                        src, srcR, srcs, srcsR = dst, dstR, dsts, dstsR
                        tot = lo
                        step *= 2
                        bi_ ^= 1
                    S.op("dve", lambda e: e.scalar_tensor_tensor(out=pooledT[:, j, 0:1024], in0=src[:, 16:1040], scalar=1.0 / w,
                                                                 in1=um[:, 16:1040], op0=ALU.mult, op1=ALU.subtract),
                         [srcR, r_um], [pooledR[j]])
                    S.op("dve", lambda e: e.scalar_tensor_tensor(out=pooledT[:, j, 1024:1152].rearrange("p (j i) -> p j i", i=8),
                                                                 in0=srcs[:, :, 15:23], scalar=1.0 / w, in1=us[:, :, 15:23],
                                                                 op0=ALU.mult, op1=ALU.subtract), [srcsR, r_us], [pooledR[j]])
                ops_ = o_pool_s.rearrange("(j r) c -> j r c", r=15)
                for i8 in range(8):
                    S.dma("sp", ops_[:, 7 + i8, 512 * gi:512 * gi + 512].rearrange("j (c q) -> j c q", q=128),
                          pst[16 * i8:16 * i8 + 16, :, :], reads=[r_pst])
                S.dma("sp", o_pool_p[:, 512 * gi:512 * gi + 512].rearrange("r (c q) -> r c q", q=128), ppt[:], reads=[r_ppt])
                for j in range(4):
                    if j % 2 == 0:
                        wcur = w_next()
                    wt, wr = wcur
                    sj = j % 2
                    for (c0, c1) in fblocks:
                        n = c1 - c0
                        pb = nps()

                        def _mm(e):
                            ins = None
                            for k in range(16):
                                ins = e.matmul(PS[pb][:, 0:n], lhsT=wt[:, k, sj * 128:(sj + 1) * 128],
                                               rhs=xnT_m[:, k, 16 + c0:16 + c1], start=(k == 0), stop=(k == 15))
                            return ins
                        S.op("pe", _mm, list(xnT_mR) + list(wr), [PSR[pb]])
                        S.op("act", lambda e: e.activation(out=zpT[:, j, c0:c1], in_=PS[pb][:, 0:n], func=AF.Silu),
                             [PSR[pb]], [zpR[j]])
                for dj in range(4):
                    if dj % 2 == 0:
                        wcur = w_next()
                    wt, wr = wcur
                    sj = dj % 2
                    c = 4 * gi + dj
                    for (c0, c1) in fblocks:
                        n = c1 - c0
                        pb = nps()

                        def _mm(e):
                            ins = None
                            for k in range(4):
                                ins = e.matmul(PS[pb][:, 0:n], lhsT=wt[:, k, sj * 128:(sj + 1) * 128], rhs=pooledT[:, k, c0:c1],
                                               start=(k == 0), stop=(k == 3))
                            return ins
                        S.op("pe", _mm, pooledR + list(wr), [PSR[pb]])
                        S.op("act", lambda e: e.activation(out=tq[:, 0:n], in_=PS[pb][:, 0:n], func=AF.Identity,
                                                           scale=cp[:, 136 + c:137 + c], bias=cp[:, 120 + c:121 + c]),
                             [PSR[pb], r_cp], [r_tq])
                        S.op("dve", lambda e: e.tensor_tensor(out=poT[:, c, c0:c1], in0=tq[:, 0:n], in1=zpT[:, dj, c0:c1], op=ALU.mult),
                             [r_tq, zpR[dj]], [poTR[c]])
            S.barrier()
        dump("poT", poT[:], [128, 16, TF], poTR, BF16)
        if stop_after == "M2":
            S.finish()
            st2.close()
            return nc, dbg_out

        mergedT = sbt(st2, "mergedT", [128, 16, TF], BF16)
        mergedR = [R() for _ in range(16)]
        sub = ExitStack()
        with sub:
            m1 = sbt(sub, "m1", [128, 2, TF], F32)
            r_m1 = R()
            sg = [sbt(sub, "sg%d" % i, [128, 512], F32) for i in range(2)]
            sgR = [R(), R()]
            psrot3 = [0]

            def nps3():
                b = psrot3[0] % 6
                psrot3[0] += 1
                return b
            for mb in range(8):
                for term in range(2):
                    wa, wra = w_next()
                    wg, wrg = w_next(keep=1)
                    act_src, act_R = (yT, yTall) if term == 0 else (poT[:], poTR)
                    for sc in range(2):
                        mc = 2 * mb + sc
                        for (c0, c1) in fblocks:
                            n = c1 - c0
                            pa_, pg_ = nps3(), nps3()

                            def _mma(e):
                                ins = None
                                for k in range(16):
                                    ins = e.matmul(PS[pa_][:, 0:n], lhsT=wa[:, k, sc * 128:(sc + 1) * 128], rhs=act_src[:, k, c0:c1],
                                                   start=(k == 0), stop=(k == 15))
                                return ins
                            S.op("pe", _mma, list(act_R) + list(wra), [PSR[pa_]])

                            def _mmg(e):
                                ins = None
                                for k in range(16):
                                    ins = e.matmul(PS[pg_][:, 0:n], lhsT=wg[:, k, sc * 128:(sc + 1) * 128],
                                                   rhs=xnT_m[:, k, 16 + c0:16 + c1], start=(k == 0), stop=(k == 15))
                                return ins
                            S.op("pe", _mmg, list(xnT_mR) + list(wrg), [PSR[pg_]])
                            sb_ = (psrot3[0] // 2) % 2
                            S.op("act", lambda e: e.activation(out=sg[sb_][:, 0:n], in_=PS[pg_][:, 0:n], func=AF.Sigmoid),
                                 [PSR[pg_]], [sgR[sb_]])
                            if term == 0:
                                S.op("dve", lambda e: e.tensor_tensor(out=m1[:, sc, c0:c1], in0=sg[sb_][:, 0:n], in1=PS[pa_][:, 0:n],
                                                                      op=ALU.mult), [sgR[sb_], PSR[pa_]], [r_m1])
                            else:
                                S.op("dve", lambda e: e.tensor_tensor(out=sg[sb_][:, 0:n], in0=sg[sb_][:, 0:n], in1=PS[pa_][:, 0:n],
                                                                      op=ALU.mult), [sgR[sb_], PSR[pa_]], [sgR[sb_]])
                                S.op("dve", lambda e: e.tensor_tensor(out=mergedT[:, mc, c0:c1], in0=sg[sb_][:, 0:n], in1=m1[:, sc, c0:c1],
                                                                      op=ALU.add), [sgR[sb_], r_m1], [mergedR[mc]])
            S.barrier()
        dump("mergedT", mergedT[:], [128, 16, TF], mergedR, BF16)
        if stop_after == "F":
            S.finish()
            st2.close()
            return nc, dbg_out

        sub = ExitStack()
        with sub:
            hn = xy[:, 0:9 * 2 * D].bitcast(F32).rearrange("p (t d) -> p t d", d=D)
            hnR = [R() for _ in range(9)]
            fnw = sbt(sub, "fnw_bc", [128, D], F32)
            r_fnw = R()
            S.dma("sp", fnw[:], fnw_d.partition_broadcast(128), writes=[r_fnw])
            junk = sbt(sub, "junk_g", [128, D], BF16)
            r_junk = R()
            gs = sbt(sub, "gs", [128, 4, 16], F32)
            r_gs = R()
            S.op("dve", lambda e: e.memset(gs[:], 0.0), [], [r_gs])
            for tt in range(9):
                S.dma("sp", hn[:, tt, :], xin[16 + 128 * tt:16 + 128 * tt + 128, :], writes=[hnR[tt]])
            psrot4 = [0]
            for db in range(8):
                wt, wr = w_next()
                for tt in range(9):
                    pb = psrot4[0] % 6
                    psrot4[0] += 1

                    def _mm(e):
                        ins = None
                        for k in range(16):
                            ins = e.matmul(PS[pb][:, 0:256], lhsT=mergedT[:, k, tt * 128:(tt + 1) * 128], rhs=wt[:, k, 0:256],
                                           start=(k == 0), stop=(k == 15))
                        return ins
                    S.op("pe", _mm, mergedR + list(wr), [PSR[pb]])
                    S.op("dve", lambda e: e.tensor_tensor(out=hn[:, tt, db * 256:(db + 1) * 256], in0=hn[:, tt, db * 256:(db + 1) * 256],
                                                          in1=PS[pb][:, 0:256], op=ALU.add), [hnR[tt], PSR[pb]], [hnR[tt]])
            for tt in range(9):
                S.op("act", lambda e: e.activation(out=junk[:], in_=hn[:, tt, :], func=AF.Square, accum_out=gs[:, 0, tt:tt + 1]),
                     [hnR[tt]], [r_junk, r_gs])
            S.op("dve", lambda e: e.tensor_scalar(out=gs[:, 1, 0:9], in0=gs[:, 0, 0:9], scalar1=1.0 / D, scalar2=EPS,
                                                  op0=ALU.mult, op1=ALU.add), [r_gs], [r_gs])
            S.op("act", lambda e: e.activation(out=gs[:, 2, 0:9], in_=gs[:, 1, 0:9], func=AF.Sqrt), [r_gs], [r_gs])
            S.op("dve", lambda e: e.reciprocal(gs[:, 3, 0:9], gs[:, 2, 0:9]), [r_gs], [r_gs])
            for tt in range(9):
                S.op("dve", lambda e: e.scalar_tensor_tensor(out=hn[:, tt, :], in0=hn[:, tt, :], scalar=gs[:, 3, tt:tt + 1], in1=fnw[:],
                                                             op0=ALU.mult, op1=ALU.mult), [hnR[tt], r_gs, r_fnw], [hnR[tt]])
                S.dma("sp", y_d[128 * tt:128 * tt + 128, :], hn[:, tt, :], reads=[hnR[tt]])
            S.barrier()
        st2.close()

        S.finish()
    return nc, dbg_out


def prep_inputs(x_prompt, x_sample, state_conv, state_ssm, state_pool, meta_tokens, norm_w, w_in,
                conv_w, conv_b, dt_bias, a_log, d_skip, ssd_norm_w, w_proj_ssd, pool_mix_w,
                pool_mix_b, pool_scale, w_proj_pool, w_out, final_norm_w):
    f = np.float32
    cst, mT = _consts()
    shared = {
        "w_in": np.ascontiguousarray(w_in[0], f), "w_ps": np.ascontiguousarray(w_proj_ssd[0], f),
        "w_pp": np.ascontiguousarray(w_proj_pool[0], f), "w_out": np.ascontiguousarray(w_out[0], f),
        "pmw": np.ascontiguousarray(pool_mix_w[0], f), "norm_w": np.ascontiguousarray(norm_w[0], f),
        "fnw": np.ascontiguousarray(final_norm_w, f), "ssdnw": np.ascontiguousarray(ssd_norm_w[0], f),
        "hvec": np.ascontiguousarray(np.stack([dt_bias[0], a_log[0], d_skip[0]]), f),
        "colp1": np.ascontiguousarray(np.concatenate([conv_w[0].reshape(4 * 24, 128), conv_b[0].reshape(24, 128)]), f),
        "colp2": np.ascontiguousarray(np.concatenate([pool_mix_b[0].reshape(16, 128), pool_scale[0].reshape(16, 128)]), f),
        "cst": cst, "maskT": np.ascontiguousarray(mT.reshape(-1)),
    }
    maps = []
    zeros_p = np.zeros((TP, D), f)
    for i in range(NCORES):
        b, hf = i // 2, i % 2
        if hf == 0:
            xin = np.concatenate([meta_tokens, x_prompt[b, 0:1024], x_sample[16 * i:16 * i + 16].reshape(128, D), zeros_p])
            fl = np.tile(np.array([[1.0, 0.0]], f), (128, 1))
        else:
            xin = np.concatenate([x_prompt[b, 1008:1024], x_prompt[b, 1024:2048], x_sample[16 * i:16 * i + 16].reshape(128, D),
                                  meta_tokens, x_prompt[b, 0:1024]])
            fl = np.tile(np.array([[0.0, 1.0]], f), (128, 1))
        m = dict(shared)
        m["xin"] = np.ascontiguousarray(xin, f)
        m["flags"] = fl
        m["sconv"] = np.ascontiguousarray(state_conv[0, 16 * i:16 * i + 16].reshape(48, 3072), f)
        m["spool"] = np.ascontiguousarray(state_pool[0, 16 * i:16 * i + 16].reshape(240, D), f)
        m["sssm"] = np.ascontiguousarray(state_ssm[0, 16 * i:16 * i + 16].reshape(16, D, 128), f)
        maps.append(m)
    return maps


_NC_CACHE = {}


def kernel(**inputs):
    inputs = {k: np.asarray(v) for k, v in inputs.items()}
    maps = prep_inputs(**inputs)
    if "nc" not in _NC_CACHE:
        _NC_CACHE["nc"] = build_program()[0]
    nc = _NC_CACHE["nc"]
    res = run_bass_kernel_spmd(nc, maps, core_ids=list(range(NCORES)))
    f = np.float32
    y_prompt = np.zeros((4, 2048, D), f)
    y_sample = np.zeros((128, 8, D), f)
    ncp = np.zeros((1, 4, 3, 3072), f)
    nsp = np.zeros((1, 4, NH, 64, 128), f)
    npp = np.zeros((1, 4, 15, D), f)
    ncs = np.zeros((1, 128, 3, 3072), f)
    nss = np.zeros((1, 128, NH, 64, 128), f)
    nps_ = np.zeros((1, 128, 15, D), f)
    for i, r in enumerate(res.results):
        b, hf = i // 2, i % 2
        y = np.asarray(r["y"])
        y_prompt[b, 1024 * hf:1024 * hf + 1024] = y[:1024]
        y_sample[16 * i:16 * i + 16] = y[1024:].reshape(16, 8, D)
        if hf == 1:
            ncp[0, b] = np.asarray(r["o_conv_p"])
            nsp[0, b] = np.asarray(r["o_ssm_p"]).reshape(NH, 64, 128)
            npp[0, b] = np.asarray(r["o_pool_p"])
        ncs[0, 16 * i:16 * i + 16] = np.asarray(r["o_conv_s"]).reshape(16, 3, 3072)
        nss[0, 16 * i:16 * i + 16] = np.asarray(r["o_ssm_s"]).reshape(16, NH, 64, 128)
        nps_[0, 16 * i:16 * i + 16] = np.asarray(r["o_pool_s"]).reshape(16, 15, D)
    return (y_prompt, y_sample, ncp, nsp, npp, ncs, nss, nps_)
```
